# Optimizing a Trainium2 kernel written in Bass

```python
import math
import jax, jax.numpy as jnp
from jax import lax
import numpy as np

D_MODEL = 1024
BATCH = 2
SEQ = 16384
DEPTH = 2

CHUNK = 64
QBLOCK = 128
MAX_STREAM_OFFSET = 4096

D_SSD = D_MODEL
SSD_HEADDIM = 64
SSD_HEADS = D_SSD // SSD_HEADDIM
SSD_GROUPS = 4
D_STATE = 128
SSD_CONV = 4
SSD_CHUNK = CHUNK

D_CONV = D_MODEL
CONF_KERNEL = 31

DIFF_HEADS = 8
DIFF_DK = 64
DIFF_DV = 2 * DIFF_DK
D_ATTN = DIFF_HEADS * DIFF_DV
ROT_DIM = DIFF_DK // 4
ROPE_THETA = 500000.0

EPS = 1e-6
LN_EPS = 1e-5

N_EVEN = (DEPTH + 1) // 2
N_ODD = DEPTH // 2

XBC_DIM = D_SSD + 2 * SSD_GROUPS * D_STATE
EVEN_SPLITS = [D_SSD, XBC_DIM, SSD_HEADS, D_CONV, D_CONV, D_CONV]
EVEN_IN = sum(EVEN_SPLITS)
EVEN_MIX = D_SSD + D_CONV
ODD_IN = 4 * D_ATTN

kernel_name = "hybrid_ssd_conformer_diffattn_trunk"


def rmsnorm(x, w, eps=EPS):
    xf = x.astype(jnp.float32)
    y = xf * lax.rsqrt(jnp.mean(xf * xf, axis=-1, keepdims=True) + eps)
    return (y * w.astype(jnp.float32)).astype(x.dtype)


def layernorm(x, w, b, eps=LN_EPS):
    xf = x.astype(jnp.float32)
    mu = jnp.mean(xf, axis=-1, keepdims=True)
    var = jnp.mean(jnp.square(xf - mu), axis=-1, keepdims=True)
    y = (xf - mu) * lax.rsqrt(var + eps)
    return (y * w.astype(jnp.float32) + b.astype(jnp.float32)).astype(x.dtype)


def causal_dwconv(x, w, b):
    k = w.shape[0]
    out = lax.conv_general_dilated(
        x, w[:, None, :].astype(x.dtype), window_strides=(1,), padding=[(k - 1, 0)],
        dimension_numbers=("NWC", "WIO", "NWC"), feature_group_count=x.shape[-1])
    return out + b.astype(x.dtype)


def ssd_chunked(x, dt, a, bm, cm, d_skip):
    bsz, s, h, p = x.shape
    g, n = bm.shape[2], bm.shape[3]
    e = h // g
    l = SSD_CHUNK
    nc = s // l
    xd = (x * dt[..., None]).reshape(bsz, nc, l, g, e, p)
    da = (dt * a).reshape(bsz, nc, l, g, e).transpose(0, 3, 4, 1, 2)
    cs = jnp.cumsum(da, axis=-1)
    bm = bm.reshape(bsz, nc, l, g, n)
    cm = cm.reshape(bsz, nc, l, g, n)
    tril = jnp.tril(jnp.ones((l, l), dtype=bool))
    seg = cs[..., :, None] - cs[..., None, :]
    decay = jnp.exp(jnp.where(tril, seg, -jnp.inf))
    cb = jnp.einsum("bclgn,bcsgn->bcgls", cm, bm)
    y_diag = jnp.einsum("bcgls,bgecls,bcsgep->bclgep", cb, decay, xd)
    decay_states = jnp.exp(cs[..., -1:] - cs)
    states = jnp.einsum("bclgn,bgecl,bclgep->bcgepn", bm, decay_states, xd)
    chunk_decay = jnp.exp(cs[..., -1])

    def step(hstate, inp):
        st, dec = inp
        return hstate * dec[..., None, None] + st, hstate

    h0 = jnp.zeros((bsz, g, e, p, n), jnp.float32)
    _, prev = lax.scan(step, h0, (jnp.moveaxis(states, 1, 0), jnp.moveaxis(chunk_decay, -1, 0)))
    prev = jnp.moveaxis(prev, 0, 1)
    y_off = jnp.einsum("bclgn,bcgepn,bgecl->bclgep", cm, prev, jnp.exp(cs))
    y = (y_diag + y_off).reshape(bsz, s, h, p)
    return y + x * d_skip[:, None]


def partial_rotary(t, pos):
    half = ROT_DIM // 2
    inv = ROPE_THETA ** (-2.0 * jnp.arange(half, dtype=jnp.float32) / ROT_DIM)
    ang = pos.astype(jnp.float32)[..., None] * inv
    cos = jnp.cos(ang)[:, :, None, None, :]
    sin = jnp.sin(ang)[:, :, None, None, :]
    tf = t.astype(jnp.float32)
    t1, t2 = tf[..., :half], tf[..., half:ROT_DIM]
    out = jnp.concatenate([t1 * cos - t2 * sin, t2 * cos + t1 * sin, tf[..., ROT_DIM:]], axis=-1)
    return out.astype(t.dtype)


def diff_attention(q, k, v, lam):
    bsz, s, h, _, dk = q.shape
    nb = s // QBLOCK
    qb = q.reshape(bsz, nb, QBLOCK, h, 2, dk).swapaxes(0, 1)
    kf = k.astype(jnp.float32)
    key_chunk = jnp.arange(s) // CHUNK
    scale = dk ** -0.5

    def one_block(args):
        qblk, bi = args
        sc = jnp.einsum("bqhmd,bkhmd->bhmqk", qblk.astype(jnp.float32), kf) * scale
        q_chunk = (bi * QBLOCK + jnp.arange(QBLOCK)) // CHUNK
        allowed = key_chunk[None, :] <= q_chunk[:, None]
        sc = jnp.where(allowed, sc, -jnp.inf)
        pr = jax.nn.softmax(sc, axis=-1)
        wts = pr[:, :, 0] - lam * pr[:, :, 1]
        return jnp.einsum("bhqk,bkhe->bqhe", wts.astype(v.dtype), v)

    o = lax.map(one_block, (qb, jnp.arange(nb)))
    return o.swapaxes(0, 1).reshape(bsz, s, h, v.shape[-1])


def even_layer(x, norm_w, w_in, conv_w, conv_b, dt_bias, a_log, d_skip, ssd_norm_w,
               dw_w, dw_b, ln_w, ln_b, w_out):
    bsz, s, _ = x.shape
    hn = rmsnorm(x, norm_w)
    proj = hn @ w_in
    idx = list(np.cumsum(EVEN_SPLITS)[:-1])
    z_a, xbc, dt_raw, glu_v, glu_g, z_b = jnp.split(proj, idx, axis=-1)
    xbc = jax.nn.silu(causal_dwconv(xbc, conv_w, conv_b))
    xs, bs, cs = jnp.split(xbc, [D_SSD, D_SSD + SSD_GROUPS * D_STATE], axis=-1)
    xs = xs.reshape(bsz, s, SSD_HEADS, SSD_HEADDIM).astype(jnp.float32)
    bs = bs.reshape(bsz, s, SSD_GROUPS, D_STATE).astype(jnp.float32)
    cs = cs.reshape(bsz, s, SSD_GROUPS, D_STATE).astype(jnp.float32)
    dt = jax.nn.softplus(dt_raw.astype(jnp.float32) + dt_bias.astype(jnp.float32))
    a = -jnp.exp(a_log.astype(jnp.float32))
    y = ssd_chunked(xs, dt, a, bs, cs, d_skip.astype(jnp.float32)).reshape(bsz, s, D_SSD)
    y = y * jax.nn.silu(z_a.astype(jnp.float32))
    y_a = rmsnorm(y.reshape(bsz, s, SSD_GROUPS, D_SSD // SSD_GROUPS),
                  ssd_norm_w.reshape(SSD_GROUPS, D_SSD // SSD_GROUPS)).reshape(bsz, s, D_SSD)
    y_a = y_a.astype(x.dtype)
    u = glu_v * jax.nn.sigmoid(glu_g)
    u = causal_dwconv(u, dw_w, dw_b)
    u = jax.nn.silu(layernorm(u, ln_w, ln_b))
    y_b = (u * jax.nn.silu(z_b)).astype(x.dtype)
    return x + jnp.concatenate([y_a, y_b], axis=-1) @ w_out


def odd_layer(x, positions, layer_idx, norm_w, w_in, q_norm_w, k_norm_w,
              lq1, lk1, lq2, lk2, subln_w, w_out):
    bsz, s, _ = x.shape
    lam_init = 0.8 - 0.6 * math.exp(-0.3 * layer_idx)
    hn = rmsnorm(x, norm_w)
    proj = hn @ w_in
    q, k, v, gate = jnp.split(proj, 4, axis=-1)
    q = rmsnorm(q.reshape(bsz, s, DIFF_HEADS, 2, DIFF_DK), q_norm_w)
    k = rmsnorm(k.reshape(bsz, s, DIFF_HEADS, 2, DIFF_DK), k_norm_w)
    q = partial_rotary(q, positions)
    k = partial_rotary(k, positions)
    v = v.reshape(bsz, s, DIFF_HEADS, DIFF_DV)
    f32 = jnp.float32
    lam = (jnp.exp(jnp.sum(lq1.astype(f32) * lk1.astype(f32)))
           - jnp.exp(jnp.sum(lq2.astype(f32) * lk2.astype(f32))) + lam_init)
    o = diff_attention(q, k, v, lam)
    o = rmsnorm(o, subln_w) * (1.0 - lam_init)
    o = o.reshape(bsz, s, D_ATTN) * jax.nn.silu(gate)
    return x + o.astype(x.dtype) @ w_out


def setup_inputs(seed: int = 0) -> dict:
    key = jax.random.key(seed)
    ks = iter(jax.random.split(key, 40))

    def nrm(shape, scale):
        return jax.random.normal(next(ks), shape, jnp.float32) * scale

    x = nrm((BATCH, SEQ, D_MODEL), 1.0)
    offset = jax.random.randint(next(ks), (BATCH, 1), 0, MAX_STREAM_OFFSET, dtype=jnp.int32)
    positions = (offset + jnp.arange(SEQ, dtype=jnp.int32)[None, :]).astype(jnp.int32)

    ne, no = N_EVEN, N_ODD
    u = jax.random.uniform(next(ks), (ne, SSD_HEADS), jnp.float32)
    dt0 = jnp.exp(u * (math.log(0.1) - math.log(0.001)) + math.log(0.001))
    a_dt_bias = dt0 + jnp.log(-jnp.expm1(-dt0))
    a_a_log = jnp.log(jax.random.uniform(next(ks), (ne, SSD_HEADS), jnp.float32, 1.0, 16.0))
    return {
        "x": x,
        "positions": positions,
        "a_norm_w": 1.0 + nrm((ne, D_MODEL), 0.05),
        "a_w_in": nrm((ne, D_MODEL, EVEN_IN), D_MODEL ** -0.5),
        "a_conv_w": nrm((ne, SSD_CONV, XBC_DIM), SSD_CONV ** -0.5),
        "a_conv_b": nrm((ne, XBC_DIM), 0.02),
        "a_dt_bias": a_dt_bias,
        "a_a_log": a_a_log,
        "a_d_skip": 1.0 + nrm((ne, SSD_HEADS), 0.05),
        "a_ssd_norm_w": 1.0 + nrm((ne, D_SSD), 0.05),
        "a_dw_w": nrm((ne, CONF_KERNEL, D_CONV), CONF_KERNEL ** -0.5),
        "a_dw_b": nrm((ne, D_CONV), 0.02),
        "a_ln_w": 1.0 + nrm((ne, D_CONV), 0.05),
        "a_ln_b": nrm((ne, D_CONV), 0.02),
        "a_w_out": nrm((ne, EVEN_MIX, D_MODEL), EVEN_MIX ** -0.5),
        "c_norm_w": 1.0 + nrm((no, D_MODEL), 0.05),
        "c_w_in": nrm((no, D_MODEL, ODD_IN), D_MODEL ** -0.5),
        "c_q_norm_w": 1.0 + nrm((no, DIFF_DK), 0.05),
        "c_k_norm_w": 1.0 + nrm((no, DIFF_DK), 0.05),
        "c_lq1": nrm((no, DIFF_DK), 0.1),
        "c_lk1": nrm((no, DIFF_DK), 0.1),
        "c_lq2": nrm((no, DIFF_DK), 0.1),
        "c_lk2": nrm((no, DIFF_DK), 0.1),
        "c_subln_w": 1.0 + nrm((no, DIFF_DV), 0.05),
        "c_w_out": nrm((no, D_ATTN, D_MODEL), D_ATTN ** -0.5),
    }


def reference(x, positions, a_norm_w, a_w_in, a_conv_w, a_conv_b, a_dt_bias, a_a_log,
              a_d_skip, a_ssd_norm_w, a_dw_w, a_dw_b, a_ln_w, a_ln_b, a_w_out,
              c_norm_w, c_w_in, c_q_norm_w, c_k_norm_w, c_lq1, c_lk1, c_lq2, c_lk2,
              c_subln_w, c_w_out):
    for i in range(DEPTH):
        if i % 2 == 0:
            j = i // 2
            x = even_layer(x, a_norm_w[j], a_w_in[j], a_conv_w[j], a_conv_b[j], a_dt_bias[j],
                           a_a_log[j], a_d_skip[j], a_ssd_norm_w[j], a_dw_w[j], a_dw_b[j],
                           a_ln_w[j], a_ln_b[j], a_w_out[j])
        else:
            j = i // 2
            x = odd_layer(x, positions, i, c_norm_w[j], c_w_in[j], c_q_norm_w[j], c_k_norm_w[j],
                          c_lq1[j], c_lk1[j], c_lq2[j], c_lk2[j], c_subln_w[j], c_w_out[j])
    return x
```

```python
import contextlib
import math
import numpy as np
import ml_dtypes
import concourse.bass as bass
import concourse.mybir as mybir
from concourse.bass_utils import run_bass_kernel_spmd

F32 = mybir.dt.float32
BF16 = mybir.dt.bfloat16
I32 = mybir.dt.int32
AF = mybir.ActivationFunctionType
ALU = mybir.AluOpType
AX = mybir.AxisListType

D = 1024
EPS = 1e-6
LN_EPS = 1e-5
TILE = 512
HALO = 32
NB_DMA = 12
EPOCH = 30000


class Prog:
    ENGS = ("pe", "act", "dve", "pool", "sp")

    def __init__(self, nc, stack):
        self.nc = nc
        self.ops = {e: [] for e in self.ENGS}
        self.count = {e: 0 for e in self.ENGS}
        self.last_w = {}
        self.readers = {}
        self.seen = {e: {} for e in self.ENGS}
        self.sems = {}
        self.stack = stack
        self.dma_tot = {}
        self.dma_rr = {e: 0 for e in self.ENGS}
        self.cc_count = 0
        self.nops = 0

    def sem(self, key):
        if key not in self.sems:
            self.sems[key] = self.stack.enter_context(self.nc.semaphore("s_%s_%s" % key))
        return self.sems[key]

    def _deps(self, eng, reads, writes):
        w = {}

        def add(d):
            for k, v in d.items():
                if w.get(k, 0) < v:
                    w[k] = v
        for t in reads:
            add(self.last_w.get(t, {}))
        for t in writes:
            add(self.last_w.get(t, {}))
            add(self.readers.get(t, {}))
        out = []
        for k, v in w.items():
            if eng == "pe" and k[0] == "pe":
                continue
            if self.seen[eng].get(k, 0) >= v:
                continue
            self.seen[eng][k] = v
            out.append((k, v))
        return out

    def _record(self, ev, reads, writes):
        k, v = ev
        for t in reads:
            d = self.readers.setdefault(t, {})
            d[k] = max(d.get(k, 0), v)
        for t in writes:
            d = self.last_w.setdefault(t, {})
            d[k] = max(d.get(k, 0), v)

    def op(self, eng, fn, reads, writes):
        reads = [r.name if hasattr(r, "name") else r for r in reads]
        writes = [r.name if hasattr(r, "name") else r for r in writes]
        waits = self._deps(eng, reads, writes)
        c = self.count[eng]
        self.count[eng] += 1
        key = (eng, c // EPOCH)
        ev = (key, c % EPOCH + 1)
        self.sem(key)
        self.seen[eng][key] = ev[1]
        self.ops[eng].append((waits, fn, key, 1))
        self._record(ev, reads, writes)
        self.nops += 1

    def dma(self, q, out, in_, extra_reads=(), extra_writes=()):
        reads = [in_.name] + list(extra_reads)
        writes = [out.name] + list(extra_writes)
        waits = self._deps(q, reads, writes)
        slot = q + str(self.dma_rr[q])
        self.dma_rr[q] = (self.dma_rr[q] + 1) % NB_DMA
        key = ("dma", slot)
        self.sem(key)
        prev = self.dma_tot.get(slot, 0)
        if prev > 0 and self.seen[q].get(key, 0) < prev:
            waits.append((key, prev))
            self.seen[q][key] = prev
        self.dma_tot[slot] = prev + 16
        ev = (key, prev + 16)
        self.ops[q].append((waits, lambda e, o=out, i=in_: e.dma_start(out=o, in_=i), key, 16))
        self._record(ev, reads, writes)
        self.nops += 1

    def collective(self, kind, groups, in_t, out_t):
        q = "pool"
        reads = [in_t.name]
        writes = [out_t.name]
        waits = self._deps(q, reads, writes)
        key = ("cc", 0)
        self.sem(key)
        self.cc_count += 1
        ev = (key, self.cc_count)

        def fn(e):
            return e.collective_compute(kind, ALU.bypass, replica_groups=groups,
                                        ins=[in_t.ap().opt()], outs=[out_t.ap().opt()])
        self.ops[q].append((waits, fn, key, 1))
        self._record(ev, reads, writes)

    def barrier(self, with_cc=False):
        allev = {}
        for e in self.ENGS:
            c = self.count[e]
            if c > 0:
                allev[(e, (c - 1) // EPOCH)] = (c - 1) % EPOCH + 1
        for s_, v_ in self.dma_tot.items():
            if v_ > 0:
                allev[("dma", s_)] = v_
        if self.cc_count and with_cc:
            allev[("cc", 0)] = self.cc_count
        for e in self.ENGS:
            waits = []
            for k, v in allev.items():
                if k[0] == e:
                    continue
                if self.seen[e].get(k, 0) >= v:
                    continue
                self.seen[e][k] = v
                waits.append((k, v))
            if waits:
                self.ops[e].append((waits, None, None, 0))

    def flush(self, final=False):
        nc = self.nc
        if final:
            self.barrier(with_cc=True)
        ops = self.ops
        sems = self.sems
        with nc.Block() as block:
            def emit(eng_handle, lst):
                for waits, fn, key, amt in lst:
                    for k, v in waits:
                        eng_handle.wait_ge(sems[k], v)
                    if fn is not None:
                        ins = fn(eng_handle)
                        ins.then_inc(sems[key], amt)

            @block.tensor
            def _(e):
                emit(e, ops["pe"])

            @block.scalar
            def _(e):
                emit(e, ops["act"])

            @block.vector
            def _(e):
                emit(e, ops["dve"])

            @block.gpsimd
            def _(e):
                emit(e, ops["pool"])

            @block.sync
            def _(e):
                emit(e, ops["sp"])
        self.ops = {e: [] for e in self.ENGS}

    def mm(self, out, lhsT, rhs, start, stop, skip=False):
        rd, wr = [lhsT, rhs], [out]
        out, lhsT, rhs = U(out), U(lhsT), U(rhs)
        if rd is not None:
            self.op("pe", lambda e: e.matmul(out, lhsT=lhsT, rhs=rhs, start=start, stop=stop), rd, wr)
            return
        if skip:
            self.op("pe", lambda e: e.matmul(out, lhsT=lhsT, rhs=rhs, start=start, stop=stop,
                                             skip_group_check=True), [lhsT, rhs], [out])
        else:
            self.op("pe", lambda e: e.matmul(out, lhsT=lhsT, rhs=rhs, start=start, stop=stop),
                    [lhsT, rhs], [out])

    def tr(self, out, in_, ident):
        rd, wr = [in_, ident], [out]
        out, in_, ident = U(out), U(in_), U(ident)
        self.op("pe", lambda e: e.transpose(out=out, in_=in_, identity=ident), rd, wr)

    def act(self, out, in_, func, bias=None, scale=None, extra=()):
        kw = {}
        rd = [in_] + list(extra)
        wr = [out]
        if bias is not None:
            kw["bias"] = U(bias)
            if hasattr(bias, "name"):
                rd.append(bias)
        if scale is not None:
            kw["scale"] = U(scale)
            if hasattr(scale, "name"):
                rd.append(scale)
        out, in_ = U(out), U(in_)
        self.op("act", lambda e: e.activation(out=out, in_=in_, func=func, **kw), rd, wr)

    def tt(self, eng, out, in0, in1, op):
        rd, wr = [in0, in1], [out]
        out, in0, in1 = U(out), U(in0), U(in1)
        self.op(eng, lambda e: e.tensor_tensor(out=out, in0=in0, in1=in1, op=op), rd, wr)

    def ts(self, eng, out, in0, s1, s2, op0, op1=None):
        rd = [in0]
        wr = [out]
        for s in (s1, s2):
            if hasattr(s, "name"):
                rd.append(s)
        out, in0, s1, s2 = U(out), U(in0), U(s1), U(s2)
        if op1 is None:
            self.op(eng, lambda e: e.tensor_scalar(out=out, in0=in0, scalar1=s1, scalar2=None, op0=op0),
                    rd, wr)
        else:
            self.op(eng, lambda e: e.tensor_scalar(out=out, in0=in0, scalar1=s1, scalar2=s2, op0=op0,
                                                   op1=op1), rd, wr)

    def stt(self, eng, out, in0, scalar, in1, op0, op1):
        rd = [in0, in1]
        wr = [out]
        if hasattr(scalar, "name"):
            rd.append(scalar)
        out, in0, scalar, in1 = U(out), U(in0), U(scalar), U(in1)
        self.op(eng, lambda e: e.scalar_tensor_tensor(out=out, in0=in0, scalar=scalar, in1=in1,
                                                      op0=op0, op1=op1), rd, wr)

    def cp(self, eng, out, in_):
        rd, wr = [in_], [out]
        out, in_ = U(out), U(in_)
        if eng == "act":
            self.op(eng, lambda e: e.copy(out=out, in_=in_), rd, wr)
            return
        if eng != "act":
            self.op(eng, lambda e: e.tensor_copy(out=out, in_=in_), rd, wr)
            return
        if eng == "act":
            self.op(eng, lambda e: e.copy(out=out, in_=in_), [in_], [out])
        else:
            self.op(eng, lambda e: e.tensor_copy(out=out, in_=in_), [in_], [out])

    def red(self, eng, out, in_, op=None):
        rd, wr = [in_], [out]
        out, in_ = U(out), U(in_)
        self.op(eng, lambda e: e.tensor_reduce(out=out, in_=in_, axis=AX.X, op=op or ALU.add), rd, wr)

    def recip(self, eng, out, in_):
        rd, wr = [in_], [out]
        out, in_ = U(out), U(in_)
        self.op(eng, lambda e: e.reciprocal(out=out, in_=in_), rd, wr)

    def memset(self, eng, out, val):
        self.op(eng, lambda e: e.memset(out, val), [], [out])


class R:
    def __init__(self, ap, tok):
        self._ap = ap
        self.name = tok

    def __getitem__(self, k):
        return R(self._ap[k], self.name)

    def rearrange(self, *a, **k):
        return R(self._ap.rearrange(*a, **k), self.name)

    def unsqueeze(self, *a):
        return R(self._ap.unsqueeze(*a), self.name)

    def to_broadcast(self, *a):
        return R(self._ap.to_broadcast(*a), self.name)


def U(x):
    return x._ap if isinstance(x, R) else x


class Alloc:
    def __init__(self, nc, stack):
        self.nc = nc
        self.stack = stack
        self.n = 0

    def sb(self, name, shape, dt):
        return self.stack.enter_context(self.nc.sbuf_tensor(name, list(shape), dt))

    def ps(self, name, shape, dt):
        return self.stack.enter_context(self.nc.psum_tensor(name, list(shape), dt))


def bc(ap, shape):
    return ap.to_broadcast(list(shape))


def front_end_g(p, xt, junk, ss, xn, tp_ps, hnT, col0, ident, ntok=128, cp_eng="act", lnexp=False):
    p.tt("dve", junk[0:ntok, :], xt[0:ntok, :], xt[0:ntok, :], ALU.mult)
    yield
    p.red("dve", ss[0:ntok, 0:1], junk[0:ntok, :])
    yield
    if lnexp:
        p.act(ss[0:ntok, 1:2], ss[0:ntok, 0:1], AF.Ln, bias=EPS_AP[0][0:ntok, :], scale=1.0 / D)
        yield
        p.ts("dve", ss[0:ntok, 3:4], ss[0:ntok, 1:2], -0.5, None, ALU.mult)
        yield
        p.act(ss[0:ntok, 2:3], ss[0:ntok, 3:4], AF.Exp)
        yield
        p.ts("dve", xn[0:ntok, :], xt[0:ntok, :], ss[0:ntok, 2:3], None, ALU.mult)
        yield
    else:
        p.act(ss[0:ntok, 1:2], ss[0:ntok, 0:1], AF.Sqrt, bias=EPS_AP[0][0:ntok, :], scale=1.0 / D)
        yield
        p.recip("dve", ss[0:ntok, 2:3], ss[0:ntok, 1:2])
        yield
        p.act(xn[0:ntok, :], xt[0:ntok, :], AF.Copy, scale=ss[0:ntok, 2:3])
        yield
    for k in range(8):
        p.tr(tp_ps[:, k, 0:ntok], xn[0:ntok, k * 128:(k + 1) * 128], ident[0:ntok, 0:ntok])
    yield
    p.cp(cp_eng, hnT[:, :, col0:col0 + ntok], tp_ps[:, :, 0:ntok])
    yield


def front_end(*a_, **k_):
    for _ in front_end_g(*a_, **k_):
        pass


def run_lanes(lanes, lag):
    alive = [True] * len(lanes)

    def step(q):
        if alive[q]:
            try:
                next(lanes[q])
            except StopIteration:
                alive[q] = False
    for _ in range(lag):
        step(0)
    while any(alive):
        for q in range(len(lanes)):
            step(q)


EPS_AP = [None]


LW_CNT = [0]


def load_weight_bf16(p, q, dst, w_dram, k, ncols, stg, scale_ap, col0=0, dst_col0=0):
    n = LW_CNT[0]
    LW_CNT[0] += 1
    if isinstance(stg, (list, tuple)):
        stg = stg[n % len(stg)]
    q = "sp" if n % 2 == 0 else "pool"
    p.dma(q, stg[:, 0:ncols], w_dram[k * 128:(k + 1) * 128, col0:col0 + ncols])
    if scale_ap is not None:
        if n % 2 == 0:
            p.act(dst[:, k, dst_col0:dst_col0 + ncols], stg[:, 0:ncols], AF.Copy, scale=scale_ap)
        else:
            p.ts("dve", dst[:, k, dst_col0:dst_col0 + ncols], stg[:, 0:ncols], scale_ap, None, ALU.mult)
    else:
        p.cp("act" if n % 2 == 0 else "dve", dst[:, k, dst_col0:dst_col0 + ncols], stg[:, 0:ncols])


def build(S, stop_after="E", debug=False):
    NT = S // TILE
    NTL = NT // 4
    NOWN = NTL * TILE
    nc = bass.Bass("TRN2", target_bir_lowering=False)
    dbg_kind = "ExternalOutput" if debug else "Internal"

    def din(name, shape, dt=F32):
        return nc.dram_tensor(name, list(shape), dt, kind="ExternalInput")

    xb = din("xb", [S, D])
    xo = din("xo", [NTL, TILE + HALO, D])
    pos = din("pos", [128, NTL * 4], F32)
    cst = din("cst", [128, 1024])
    w_ssd = din("w_ssd", [D, 772])
    vec_a = din("vec_a", [128, 64])
    bc_a = din("bc_a", [64, 512])
    w_conf = din("w_conf", [D, 3072])
    dw_w = din("dw_w", [128, 8 * 31])
    w_out0 = din("w_out0", [2048, D])
    w_in1 = din("w_in1", [D, 4096])
    bc_c = din("bc_c", [128, 512])
    w_out1 = din("w_out1", [D, D])
    masks = din("masks", [128, 16 * 512], BF16)
    flags = din("flags", [128, 8])
    out = nc.dram_tensor("out", [NOWN, D], F32, kind="ExternalOutput")

    CW = min(S, 2048)
    NCA = S // CW
    yaT_in = [nc.dram_tensor("yaT_in%d" % i, [256, CW], BF16) for i in range(NCA)]
    yaT_all = [nc.dram_tensor("yaT_all%d" % i, [1024, CW], BF16) for i in range(NCA)]
    x1s = nc.dram_tensor("x1s", [NOWN, D], F32, kind=dbg_kind) if debug else nc.dram_tensor("x1s", [NOWN, D], F32)
    QTs = nc.dram_tensor("QTs", [1024, NOWN], BF16)
    gs = nc.dram_tensor("gs", [NOWN, 1024], BF16)
    kT_in = [nc.dram_tensor("kT_in%d" % i, [128, NOWN], BF16) for i in range(8)]
    kT_all = [nc.dram_tensor("kT_all%d" % i, [512, NOWN], BF16) for i in range(8)]
    VR = min(NOWN, 512)
    NVC = NOWN // VR
    v_in = [nc.dram_tensor("v_in%d" % i, [VR, 1024], BF16) for i in range(NVC)]
    v_all = [nc.dram_tensor("v_all%d" % i, [4 * VR, 1024], BF16) for i in range(NVC)]
    dbg = {}
    if debug:
        dbg["ya"] = nc.dram_tensor("dbg_ya", [1024, S], BF16, kind="ExternalOutput")
        dbg["yb"] = nc.dram_tensor("dbg_yb", [1024, NOWN], BF16, kind="ExternalOutput")

    groups = [[0, 1, 2, 3], [4, 5, 6, 7]]

    def dump(p, name, ap, shape, dt=F32):
        if not debug:
            return
        t = nc.dram_tensor("dd_" + name, list(shape), dt, kind="ExternalOutput")
        p.dma("pool", t[tuple(slice(0, n) for n in shape)], ap)

    with contextlib.ExitStack() as gstack:
        p = Prog(nc, gstack)
        ga = Alloc(nc, gstack)
        cst_sb = ga.sb("cst_sb", [128, 1024], F32)
        ident = ga.sb("ident", [128, 128], BF16)
        eps_t = ga.sb("eps_t", [128, 4], F32)
        vec_sb = ga.sb("vec_sb", [128, 64], F32)
        p.dma("sp", cst_sb[:, :], cst[:, :])
        p.dma("sp", vec_sb[:, :], vec_a[:, :])
        p.cp("dve", ident[:, :], cst_sb[:, 0:128])
        p.memset("dve", eps_t[:, 0:1], EPS)
        p.memset("dve", eps_t[:, 1:2], LN_EPS)
        p.memset("dve", eps_t[:, 2:3], 1.0)
        EPS_AP[0] = eps_t[:, 0:1]
        bcc = ga.sb("bcc", [128, 512], F32)
        lamt = ga.sb("lamt", [128, 8], F32)
        negM = ga.sb("negM", [128, 4], F32)
        ltmp = ga.sb("ltmp", [128, 64], F32)
        p.dma("sp", bcc[:, :], bc_c[:, :])
        LAM_INIT = 0.8 - 0.6 * math.exp(-0.3 * 1)
        p.tt("dve", ltmp[:, :], bcc[:, 128:192], bcc[:, 192:256], ALU.mult)
        p.red("dve", lamt[:, 0:1], ltmp[:, :])
        p.tt("dve", ltmp[:, :], bcc[:, 256:320], bcc[:, 320:384], ALU.mult)
        p.red("dve", lamt[:, 1:2], ltmp[:, :])
        p.act(lamt[:, 4:6], lamt[:, 0:2], AF.Exp)
        p.tt("dve", lamt[:, 2:3], lamt[:, 4:5], lamt[:, 5:6], ALU.subtract)
        p.ts("pool", lamt[:, 3:4], lamt[:, 2:3], LAM_INIT, -1.0, ALU.add, ALU.mult)
        p.ts("dve", bcc[:, 384:512], bcc[:, 384:512], 1.0 - LAM_INIT, None, ALU.mult)
        p.tt("dve", ltmp[:, :], bcc[:, 0:64], bcc[:, 0:64], ALU.mult)
        p.red("dve", negM[:, 1:2], ltmp[:, :], ALU.max)
        p.tt("dve", ltmp[:, :], bcc[:, 64:128], bcc[:, 64:128], ALU.mult)
        p.red("dve", negM[:, 2:3], ltmp[:, :], ALU.max)
        p.tt("pool", negM[:, 3:4], negM[:, 1:2], negM[:, 2:3], ALU.mult)
        p.ts("dve", ltmp[:, 0:1], negM[:, 3:4], 64.0, None, ALU.mult)
        p.act(ltmp[:, 1:2], ltmp[:, 0:1], AF.Sqrt)
        p.ts("dve", negM[:, 0:1], ltmp[:, 1:2], -1.0, None, ALU.mult)
        tri = cst_sb[0:64, 128:192]
        ones64 = cst_sb[0:64, 192:320]
        maskSU = cst_sb[0:64, 320:384]
        maskLE = cst_sb[0:64, 384:448]

        with contextlib.ExitStack() as st:
            a = Alloc(nc, st)
            Wssd = a.sb("Wssd", [128, 8, 772], BF16)
            stg = [a.sb("stgA%d" % i_, [128, 772], F32) for i_ in range(2)]
            bca = a.sb("bca", [64, 512], F32)
            xt = [a.sb("xtA%d" % i, [128, D], F32) for i in range(2)]
            xn = a.sb("xnA", [128, D], BF16)
            junk = a.sb("junkA", [128, D], F32)
            ss = a.sb("ssA", [128, 4], F32)
            hnT2 = [a.sb("hnTA%d" % i, [128, 8, TILE], BF16) for i in range(2)]
            pre = [a.sb("preA%d" % i, [128, TILE + 3], F32) for i in range(4)]
            acc = a.sb("accA", [128, TILE], F32)
            xbc2 = [[a.sb("xbcA%d_%d" % (j, i), [128, TILE], BF16) for i in range(4)] for j in range(2)]
            yaT2 = [a.sb("yaTA%d" % i, [128, 2, TILE], BF16) for i in range(2)]
            Sst = a.sb("SstA", [128, 256], F32)
            Sbf = a.sb("SbfA", [128, 256], BF16)

            class CS:
                pass
            sets = []
            tp_ps = a.ps("tp_psA", [128, 8, 128], BF16)
            bk1 = a.ps("bk1A", [128, 1024], BF16)
            bk6 = a.ps("bk6A", [128, 512], F32)
            mm_ps = [a.ps("mm_psA0", [128, TILE], F32)]
            for q in range(2):
                c = CS()
                n_ = lambda nm: "%sA_%d" % (nm, q)
                c.sz = a.sb(n_("sz"), [64, 256], F32)
                c.sze = a.sb(n_("sze"), [64, 256], F32)
                c.szr = a.sb(n_("szr"), [64, 256], F32)
                c.dtt = a.sb(n_("dtt"), [64, 32], F32)
                c.dst = a.sb(n_("dst"), [64, 8], F32)
                c.cdb = a.sb(n_("cdb"), [128, 4], F32)
                c.xtok = a.sb(n_("xtok"), [64, 256], F32)
                c.btok = a.sb(n_("btok"), [64, 128], BF16)
                c.xd = a.sb(n_("xd"), [64, 256], BF16)
                c.xdd = a.sb(n_("xdd"), [64, 256], BF16)
                c.xdf = a.sb(n_("xdf"), [64, 256], F32)
                c.cbm = a.sb(n_("cbm"), [64, 64], F32)
                c.Lh = [a.sb(n_("Lh%d" % i), [64, 64], F32) for i in range(2)]
                c.dec = [a.sb(n_("dec%d" % i), [64, 64], F32) for i in range(2)]
                c.Mh = [a.sb(n_("Mh%d" % i), [64, 64], BF16) for i in range(2)]
                c.y1 = a.sb(n_("y1"), [64, 256], F32)
                c.y3 = a.sb(n_("y3"), [64, 256], F32)
                c.ssy = a.sb(n_("ssy"), [64, 4], F32)
                c.y6 = a.sb(n_("y6"), [64, 256], BF16)
                bkA = a.ps(n_("bkzy"), [128, 512], F32)
                bkB = a.ps(n_("bkos"), [128, 512], F32)
                c.z_ps = bkA[0:64, 0:256]
                c.yd_ps = bkA[0:64, 256:512]
                c.yo_ps = bkB[0:64, 0:256]
                c.st_ps = bkB[:, 256:512]
                small = bk6[:, q * 256:q * 256 + 256]
                c.sm_ps = small[:, 0:16]
                c.cb_ps = small[0:64, 16:80]
                c.seg_ps = [small[0:64, 80:144], small[0:64, 144:208]]
                half = bk1[:, q * 512:(q + 1) * 512]
                c.xt_ps = half[0:64, 0:384]
                c.yT_ps = half[:, 384:512].rearrange("p (a b) -> p a b", a=2)
                sets.append(c)

            for k in range(8):
                load_weight_bf16(p, "sp", Wssd, w_ssd, k, 772, stg, vec_sb[:, k:k + 1])
            p.dma("sp", bca[:, :], bc_a[:, :])
            p.act(bca[:, 16:20], bca[:, 4:8], AF.Exp)
            p.ts("dve", bca[:, 4:8], bca[:, 16:20], -1.0, None, ALU.mult)
            p.memset("dve", Sst[:, :], 0.0)
            p.memset("dve", Sbf[:, :], 0.0)
            for i in range(4):
                p.memset("dve", pre[i][:, 0:3], 0.0)

            def prologue(T):
                hnT = hnT2[T % 2]
                xbc = xbc2[T % 2]
                for u in range(4):
                    x_t = xt[u % 2]
                    p.dma("sp", x_t[:, :], xb[T * TILE + u * 128:T * TILE + (u + 1) * 128, :])
                    yield from front_end_g(p, x_t, junk, ss, xn, tp_ps, hnT, u * 128, ident, lnexp=True)
                for ct in range(4):
                    mps = mm_ps[0]
                    for k in range(8):
                        p.mm(mps[:, :], Wssd[:, k, 256 + ct * 128:256 + (ct + 1) * 128], hnT[:, k, :],
                             k == 0, k == 7)
                    yield
                    p.cp("act", pre[ct][:, 3:TILE + 3], mps[:, :])
                    yield
                    cw = 8 + ct * 4
                    p.ts("dve", acc[:, :], pre[ct][:, 0:TILE], vec_sb[:, cw:cw + 1], None, ALU.mult)
                    yield
                    for j in range(1, 4):
                        p.stt("dve", acc[:, :], pre[ct][:, j:j + TILE], vec_sb[:, cw + j:cw + j + 1],
                              acc[:, :], ALU.mult, ALU.add)
                        yield
                    p.act(xbc[ct][:, :], acc[:, :], AF.Silu, bias=vec_sb[:, 24 + ct:25 + ct])
                    p.cp("dve", pre[ct][:, 0:3], pre[ct][:, TILE:TILE + 3])
                    yield

            pro = {}

            def pro_start(T):
                if T < NT and T not in pro:
                    pro[T] = prologue(T)

            def pro_step(T, n=1):
                g = pro.get(T)
                for _ in range(n):
                    if g is None:
                        return
                    try:
                        next(g)
                    except StopIteration:
                        pro[T] = None
                        return

            def pro_finish(T):
                pro_start(T)
                while pro.get(T) is not None:
                    pro_step(T)

            def chunk(cg, c):
                T, ch = cg // 8, cg % 8
                hnT = hnT2[T % 2]
                xbc = xbc2[T % 2]
                yaT = yaT2[T % 2]
                t0 = ch * 64
                tk = slice(t0, t0 + 64)
                dtt, dst, cdb = c.dtt, c.dst, c.cdb
                for k in range(8):
                    p.mm(c.z_ps[:, :], hnT[:, k, tk], Wssd[:, k, 0:256], k == 0, k == 7)
                for k in range(8):
                    p.mm(c.sm_ps[0:64, 0:4], hnT[:, k, tk], Wssd[:, k, 768:772], k == 0, k == 7)
                yield 'a'
                p.act(c.sze[:, :], c.z_ps[:, :], AF.Exp, scale=-1.0)
                p.tt("dve", dtt[:, 16:20], c.sm_ps[0:64, 0:4], bca[:, 0:4], ALU.add)
                yield 'a'
                p.ts("dve", c.sze[:, :], c.sze[:, :], 1.0, None, ALU.add)
                p.act(dtt[:, 20:24], dtt[:, 16:20], AF.Exp)
                yield 'a'
                p.recip("dve", c.szr[:, :], c.sze[:, :])
                yield 'a'
                p.ts("dve", dtt[:, 24:28], dtt[:, 20:24], 1.0, None, ALU.add)
                yield 'a'
                p.act(dtt[:, 0:4], dtt[:, 24:28], AF.Ln)
                p.tt("dve", c.sz[:, :], c.z_ps[:, :], c.szr[:, :], ALU.mult)
                yield 'a'
                p.tt("pool", dtt[:, 4:8], dtt[:, 0:4], bca[:, 4:8], ALU.mult)
                yield 'a'
                p.mm(c.sm_ps[0:64, 4:8], tri, dtt[:, 4:8], True, True)
                p.mm(c.sm_ps[0:64, 8:12], ones64[:, 0:64], dtt[:, 4:8], True, True)
                p.mm(c.sm_ps[:, 12:16], ones64, dtt[:, 4:8], True, True)
                p.tr(c.xt_ps[:, 0:128], xbc[0][:, tk], ident[:, :])
                p.tr(c.xt_ps[:, 128:256], xbc[1][:, tk], ident[:, :])
                p.tr(c.xt_ps[:, 256:384], xbc[2][:, tk], ident[:, :])
                p.mm(c.cb_ps[:, :], xbc[2][:, tk], xbc[3][:, tk], True, True)
                yield 'a'
                p.cp("act", dtt[:, 8:12], c.sm_ps[0:64, 4:8])
                p.act(dtt[:, 12:16], c.sm_ps[0:64, 4:8], AF.Exp)
                yield 'a'
                p.tt("dve", dst[:, 0:4], c.sm_ps[0:64, 8:12], dtt[:, 8:12], ALU.subtract)
                p.act(cdb[:, :], c.sm_ps[:, 12:16], AF.Exp)
                yield 'a'
                p.act(dst[:, 4:8], dst[:, 0:4], AF.Exp)
                p.tt("dve", c.cbm[:, :], c.cb_ps[:, :], maskLE, ALU.mult)
                yield 'a'
                p.cp("act", c.xtok[:, :], c.xt_ps[:, 0:256])
                yield 'a'
                p.cp("act", c.btok[:, :], c.xt_ps[:, 256:384])
                x3 = c.xtok[:, :].rearrange("p (h d) -> p h d", h=4)
                p.tt("dve", c.xdf[:, :].rearrange("p (h d) -> p h d", h=4), x3,
                     bc(dtt[:, 0:4].unsqueeze(2), [64, 4, 64]), ALU.mult)
                yield 'a'
                p.cp("dve", c.xd[:, :], c.xdf[:, :])
                p.tt("dve", c.xdd[:, :].rearrange("p (h d) -> p h d", h=4),
                     c.xdf[:, :].rearrange("p (h d) -> p h d", h=4),
                     bc(dst[:, 4:8].unsqueeze(2), [64, 4, 64]), ALU.mult)
                yield 1
                p.mm(c.yo_ps[:, :], xbc[3][:, tk], Sbf[:, :], True, True)
                p.mm(c.st_ps[:, :], c.btok[:, :], c.xdd[:, :], True, True)
                yield 'b'
                p.tt("dve", Sst[:, :].rearrange("p (h d) -> p h d", h=4),
                     Sst[:, :].rearrange("p (h d) -> p h d", h=4),
                     bc(cdb[:, 0:4].unsqueeze(2), [128, 4, 64]), ALU.mult)
                yield 'b'
                p.tt("dve", Sst[:, :], Sst[:, :], c.st_ps[:, :], ALU.add)
                yield 'b'
                p.cp("act", Sbf[:, :], Sst[:, :])
                yield 1
                for h in range(4):
                    i2 = h % 2
                    p.ts("dve", c.Lh[i2][:, :], maskSU, dtt[:, 4 + h:5 + h], None, ALU.mult)
                    yield 'c'
                    p.mm(c.seg_ps[i2][:, :], c.Lh[i2][:, :], tri, True, True)
                    yield 'c'
                    p.act(c.dec[i2][:, :], c.seg_ps[i2][:, :], AF.Exp)
                    yield 'c'
                    p.tt("dve", c.Mh[i2][:, :], c.dec[i2][:, :], c.cbm[:, :], ALU.mult)
                    yield 'c'
                    p.mm(c.yd_ps[:, h * 64:(h + 1) * 64], c.Mh[i2][:, :], c.xd[:, h * 64:(h + 1) * 64], True, True)
                    yield 'c'
                y1, y3 = c.y1, c.y3
                p.tt("dve", y1[:, :].rearrange("p (h d) -> p h d", h=4),
                     c.yo_ps[:, :].rearrange("p (h d) -> p h d", h=4),
                     bc(dtt[:, 12:16].unsqueeze(2), [64, 4, 64]), ALU.mult)
                p.tt("pool", y3[:, :].rearrange("p (h d) -> p h d", h=4), x3,
                     bc(bca[:, 8:12].unsqueeze(2), [64, 4, 64]), ALU.mult)
                yield 'c'
                p.tt("dve", y1[:, :], y1[:, :], c.yd_ps[:, :], ALU.add)
                yield 'c'
                p.tt("dve", y1[:, :], y1[:, :], y3[:, :], ALU.add)
                yield 'c'
                p.tt("dve", y1[:, :], y1[:, :], c.sz[:, :], ALU.mult)
                yield 'c'
                p.tt("dve", y3[:, :], y1[:, :], y1[:, :], ALU.mult)
                yield 'c'
                p.red("dve", c.ssy[:, 0:1], y3[:, :])
                yield 'c'
                p.act(c.ssy[:, 1:2], c.ssy[:, 0:1], AF.Ln, bias=eps_t[0:64, 0:1], scale=1.0 / 256)
                yield 'c'
                p.ts("dve", c.ssy[:, 3:4], c.ssy[:, 1:2], -0.5, None, ALU.mult)
                yield 'c'
                p.act(c.ssy[:, 2:3], c.ssy[:, 3:4], AF.Exp)
                yield 'c'
                p.ts("dve", y3[:, :], y1[:, :], c.ssy[:, 2:3], None, ALU.mult)
                yield 'c'
                p.tt("dve", c.y6[:, :], y3[:, :], bca[:, 256:512], ALU.mult)
                yield 'c'
                p.tr(c.yT_ps[:, 0, :], c.y6[:, 0:128], ident[0:64, 0:64])
                p.tr(c.yT_ps[:, 1, :], c.y6[:, 128:256], ident[0:64, 0:64])
                yield 'c'
                p.cp("act", yaT[:, :, tk], c.yT_ps[:, :, :])
                yield 1

            def store_tile(T):
                yaT = yaT2[T % 2]
                for hh in range(2):
                    c0 = (T % 4) * NOWN + (T // 4) * TILE
                    p.dma("pool", yaT_in[c0 // CW][hh * 128:(hh + 1) * 128, c0 % CW:c0 % CW + TILE], yaT[:, hh, :])

            cur = [0]

            def lane(q):
                for cg in range(q, NT * 8, 2):
                    T = cg // 8
                    if cg % 8 == q:
                        pro_finish(T)
                        if q == 1:
                            cur[0] = T + 1
                            pro_start(T + 1)
                    for v_ in chunk(cg, sets[q]):
                        yield v_
                        if q == 1:
                            pro_step(cur[0], 1)
                    if q == 1 and cg % 8 == 7:
                        store_tile(T)

            lanes = [lane(0), lane(1)]
            LAG = 24
            alive = [True, True]

            FINE = set('abc')

            def step(q):
                while alive[q]:
                    try:
                        v = next(lanes[q])
                    except StopIteration:
                        alive[q] = False
                        return
                    if v == 1 or v in FINE:
                        return
            SEQ_DEBUG = False
            LAG = 20
            if SEQ_DEBUG:
                for cg in range(NT * 8):
                    if cg % 8 == 0:
                        prologue(cg // 8)
                    for _ in chunk(cg, sets[cg % 2]):
                        pass
                    if cg % 8 == 7:
                        store_tile(cg // 8)
            else:
                for _ in range(LAG):
                    step(0)
                while alive[0] or alive[1]:
                    step(0)
                    step(1)
            for i_ in range(NCA):
                p.collective("AllGather", groups, yaT_in[i_], yaT_all[i_])
            if debug:
                for i_ in range(NCA):
                    p.dma("pool", dbg["ya"][:, i_ * CW:(i_ + 1) * CW], yaT_all[i_][:, :])
            p.barrier()
            p.flush(final=(stop_after == "A"))
        if stop_after == "A":
            return nc

        with contextlib.ExitStack() as stO:
            ao = Alloc(nc, stO)
            ybT = ao.sb("ybT", [128, 8, NOWN], BF16)
            flg = ao.sb("flg", [128, 8], F32)
            p.dma("sp", flg[:, :], flags[:, :])
            with contextlib.ExitStack() as st:
                a = Alloc(nc, st)
                Wc = a.sb("Wc", [128, 8, 3072], BF16)
                stg = [a.sb("stgB%d" % i_, [128, 1024], F32) for i_ in range(2)]
                dww = a.sb("dww", [128, 8 * 31], F32)
                xt = [a.sb("xtB%d" % i, [128, D], F32) for i in range(2)]
                xn = a.sb("xnB", [128, D], BF16)
                junk = a.sb("junkB", [128, D], F32)
                ss = a.sb("ssB", [128, 4], F32)
                hnT = a.sb("hnTB", [128, 8, TILE + HALO], BF16)
                sg = a.sb("sgB", [128, TILE + HALO], F32)
                u = a.sb("uB", [128, TILE + HALO], F32)
                uc = a.sb("ucB", [128, 8, TILE], F32)
                acc2 = a.sb("acc2B", [128, TILE], F32)
                accp = a.sb("accpB", [128, TILE], F32)
                ucb = a.sb("ucbB", [128, TILE], BF16)
                usq = a.sb("usqB", [128, TILE], BF16)
                onesb = a.sb("onesB", [128, 128], BF16)
                mean = a.sb("meanB", [128, TILE], F32)
                rstd = a.sb("rstdB", [128, TILE], F32)
                tmpv = a.sb("tmpvB", [128, TILE], F32)
                t2 = a.sb("t2B", [128, TILE], F32)
                szb = a.sb("szbB", [128, TILE], F32)
                tp_ps = a.ps("tp_psB", [128, 8, 128], BF16)
                vm_ps = a.ps("vm_psB", [128, TILE], F32)
                gm_ps = a.ps("gm_psB", [128, TILE], F32)
                h_ps = a.ps("h_psB", [128, 512], F32)
                sum_ps = a.ps("sum_psB", [128, TILE], F32)
                sq_ps = a.ps("sq_psB", [128, TILE], F32)
                zb_ps = a.ps("zb_psB", [128, TILE], F32)
                for k in range(8):
                    for c3 in range(3):
                        load_weight_bf16(p, "sp", Wc, w_conf, k, 1024, stg, vec_sb[:, k:k + 1],
                                         col0=c3 * 1024, dst_col0=c3 * 1024)
                p.dma("sp", dww[:, :], dw_w[:, :])
                p.memset("dve", onesb[:, :], 1.0)
                for i in range(NTL):
                    p.dma("sp", xt[0][0:HALO, :], xo[i, 0:HALO, :])
                    front_end(p, xt[0], junk, ss, xn, tp_ps, hnT, 0, ident, ntok=HALO)
                    for uu in range(4):
                        x_t = xt[(uu + 1) % 2]
                        p.dma("sp", x_t[:, :], xo[i, HALO + uu * 128:HALO + (uu + 1) * 128, :])
                        front_end(p, x_t, junk, ss, xn, tp_ps, hnT, HALO + uu * 128, ident)
                    for ct in range(8):
                        cv = slice(ct * 128, (ct + 1) * 128)
                        cg_ = slice(1024 + ct * 128, 1024 + (ct + 1) * 128)
                        for k in range(8):
                            p.mm(vm_ps[:, :], Wc[:, k, cv], hnT[:, k, HALO:], k == 0, k == 7)
                        for k in range(8):
                            p.mm(gm_ps[:, :], Wc[:, k, cg_], hnT[:, k, HALO:], k == 0, k == 7)
                        for k in range(8):
                            p.mm(h_ps[:, 0:HALO], Wc[:, k, cv], hnT[:, k, 0:HALO], k == 0, k == 7)
                        for k in range(8):
                            p.mm(h_ps[:, HALO:2 * HALO], Wc[:, k, cg_], hnT[:, k, 0:HALO], k == 0, k == 7)
                        p.act(sg[:, HALO:], gm_ps[:, :], AF.Sigmoid)
                        p.act(sg[:, 0:HALO], h_ps[:, HALO:2 * HALO], AF.Sigmoid)
                        p.tt("dve", u[:, HALO:], vm_ps[:, :], sg[:, HALO:], ALU.mult)
                        p.tt("dve", u[:, 0:HALO], h_ps[:, 0:HALO], sg[:, 0:HALO], ALU.mult)
                        wv = ct * 31
                        p.ts("dve", uc[:, ct, :], u[:, 2:2 + TILE], dww[:, wv:wv + 1], vec_sb[:, 28 + ct:29 + ct],
                             ALU.mult, ALU.add)
                        NPL = 5
                        for j in range(1, 31 - NPL):
                            p.stt("dve", uc[:, ct, :], u[:, 2 + j:2 + j + TILE], dww[:, wv + j:wv + j + 1],
                                  uc[:, ct, :], ALU.mult, ALU.add)
                        j0 = 31 - NPL
                        p.ts("pool", acc2[:, :], u[:, 2 + j0:2 + j0 + TILE], dww[:, wv + j0:wv + j0 + 1], None, ALU.mult)
                        for j in range(j0 + 1, 31):
                            p.ts("pool", accp[:, :], u[:, 2 + j:2 + j + TILE], dww[:, wv + j:wv + j + 1], None, ALU.mult)
                            p.tt("pool", acc2[:, :], acc2[:, :], accp[:, :], ALU.add)
                        p.tt("dve", uc[:, ct, :], uc[:, ct, :], acc2[:, :], ALU.add)
                        p.cp("act", ucb[:, :], uc[:, ct, :])
                        p.act(usq[:, :], uc[:, ct, :], AF.Square)
                        p.mm(sum_ps[:, :], onesb[:, :], ucb[:, :], ct == 0, ct == 7)
                        p.mm(sq_ps[:, :], onesb[:, :], usq[:, :], ct == 0, ct == 7)
                    p.act(mean[:, :], sum_ps[:, :], AF.Copy, scale=1.0 / 1024)
                    p.act(rstd[:, :], sq_ps[:, :], AF.Copy, scale=1.0 / 1024)
                    p.tt("dve", tmpv[:, :], mean[:, :], mean[:, :], ALU.mult)
                    p.tt("dve", rstd[:, :], rstd[:, :], tmpv[:, :], ALU.subtract)
                    p.act(rstd[:, :], rstd[:, :], AF.Sqrt, bias=eps_t[:, 1:2])
                    p.recip("dve", rstd[:, :], rstd[:, :])
                    for ct in range(8):
                        cz = slice(2048 + ct * 128, 2048 + (ct + 1) * 128)
                        for k in range(8):
                            p.mm(zb_ps[:, :], Wc[:, k, cz], hnT[:, k, HALO:], k == 0, k == 7)
                        p.tt("dve", tmpv[:, :], uc[:, ct, :], mean[:, :], ALU.subtract)
                        p.tt("dve", tmpv[:, :], tmpv[:, :], rstd[:, :], ALU.mult)
                        p.act(t2[:, :], tmpv[:, :], AF.Silu, bias=vec_sb[:, 44 + ct:45 + ct],
                              scale=vec_sb[:, 36 + ct:37 + ct])
                        p.act(szb[:, :], zb_ps[:, :], AF.Silu)
                        p.tt("dve", ybT[:, ct, i * TILE:(i + 1) * TILE], t2[:, :], szb[:, :], ALU.mult)
                if debug:
                    for ct in range(8):
                        p.dma("pool", dbg["yb"][ct * 128:(ct + 1) * 128, :], ybT[:, ct, :])
                p.barrier()
                p.flush(final=(stop_after == "B"))
            if stop_after == "B":
                return nc
            with contextlib.ExitStack() as st:
                a = Alloc(nc, st)
                Wo = a.sb("Wo0", [128, 16, D], BF16)
                stg = [a.sb("stgC%d" % i_, [128, 1024], F32) for i_ in range(2)]

                class CSet:
                    pass
                csets = []
                for q in range(2):
                    c = CSet()
                    c.cand = [a.sb("candC%d_%d" % (i, q), [128, 8, 128], BF16) for i in range(4)]
                    c.yT = a.sb("yTC%d" % q, [128, 8, 128], BF16)
                    c.yTf = a.sb("yTfC%d" % q, [128, 8, 128], F32)
                    c.x_t = a.sb("xtC%d" % q, [128, D], F32)
                    c.x1t = a.sb("x1tC%d" % q, [128, D], F32)
                    c.xn = a.sb("xnC%d" % q, [128, D], BF16)
                    c.junk = a.sb("junkC%d" % q, [128, D], F32)
                    c.ss = a.sb("ssC%d" % q, [128, 4], F32)
                    c.tp_ps = a.ps("tp_psC%d" % q, [128, 8, 128], BF16)
                    c.o_ps = [a.ps("o_psC%d_%d" % (i, q), [128, 512], F32) for i in range(2)]
                    csets.append(c)
                for k in range(16):
                    load_weight_bf16(p, "sp", Wo, w_out0, k, 1024, stg, None)

                def tileC(t, c):
                    i, uu = t // 4, t % 4
                    ybt = R(ybT[:, :, t * 128:(t + 1) * 128], "ybT_%d" % t)
                    for r in range(4):
                        c0_ = r * NOWN + t * 128
                        p.dma("sp" if r % 2 == 0 else "pool", c.cand[r][:, :, :],
                              yaT_all[c0_ // CW][:, c0_ % CW:c0_ % CW + 128].rearrange("(k p) t -> p k t", p=128))
                    p.dma("sp", c.x_t[:, :], xo[i, HALO + uu * 128:HALO + (uu + 1) * 128, :])
                    yield
                    p.ts("dve", c.yTf[:, :, :], c.cand[0][:, :, :], flg[:, 0:1], None, ALU.mult)
                    yield
                    for r in range(1, 4):
                        p.stt("dve", c.yTf[:, :, :], c.cand[r][:, :, :], flg[:, r:r + 1], c.yTf[:, :, :], ALU.mult, ALU.add)
                        yield
                    p.cp("dve", c.yT[:, :, :], c.yTf[:, :, :])
                    yield
                    for dh in range(2):
                        for kc in range(8):
                            p.mm(c.o_ps[dh][:, :], c.yT[:, kc, :], Wo[:, kc, dh * 512:(dh + 1) * 512], kc == 0, False)
                        for kc in range(8):
                            p.mm(c.o_ps[dh][:, :], ybt[:, kc, :],
                                 Wo[:, 8 + kc, dh * 512:(dh + 1) * 512], False, kc == 7)
                        yield
                        p.tt("dve", c.x1t[:, dh * 512:(dh + 1) * 512], c.o_ps[dh][:, :], c.x_t[:, dh * 512:(dh + 1) * 512], ALU.add)
                        yield
                    p.dma("pool", x1s[t * 128:(t + 1) * 128, :], c.x1t[:, :])
                    yield from front_end_g(p, c.x1t, c.junk, c.ss, c.xn, c.tp_ps, ybt, 0, ident)

                def laneC(q):
                    for t in range(q, NTL * 4, 2):
                        yield from tileC(t, csets[q])
                run_lanes([laneC(0), laneC(1)], 8)
                p.barrier()
                p.flush(final=(stop_after == "C"))
            if stop_after == "C":
                return nc

            with contextlib.ExitStack() as st:
                a = Alloc(nc, st)
                NT4 = NTL * 4
                W1 = a.sb("W1", [128, 8, 4096], BF16)
                stg = [a.sb("stgD%d" % i_, [128, 1024], F32) for i_ in range(2)]
                posf = a.sb("posf", [128, NT4], F32)
                ang = a.sb("ang", [128, NT4, 8], F32)
                tm1 = a.sb("tm1D", [128, NT4, 8], F32)
                tm2 = a.sb("tm2D", [128, NT4, 8], F32)
                tm3 = a.sb("tm3D", [128, NT4, 8], F32)
                cosT = a.sb("cosT", [128, NT4, 8], F32)
                sinT = a.sb("sinT", [128, NT4, 8], F32)

                class DSet:
                    pass
                dsets = []
                for q in range(2):
                    c = DSet()
                    c.qk = a.sb("qkD%d" % q, [128, 2048], F32)
                    c.sq = a.sb("sqD%d" % q, [128, 2048], BF16)
                    c.ssq = a.sb("ssqD%d" % q, [128, 96], F32)
                    c.qkb = a.sb("qkbD%d" % q, [128, 2048], BF16)
                    c.vt = a.sb("vtD%d" % q, [128, 1024], BF16)
                    c.gt = a.sb("gtD%d" % q, [128, 1024], BF16)
                    c.rA = a.sb("rA%d" % q, [128, 32, 8], F32)
                    c.rB = a.sb("rB%d" % q, [128, 32, 8], F32)
                    c.rA2 = a.sb("rA2%d" % q, [128, 32, 8], F32)
                    c.rB2 = a.sb("rB2%d" % q, [128, 32, 8], F32)
                    c.QTt = a.sb("QTt%d" % q, [128, 8, 128], BF16)
                    c.KTt = a.sb("KTt%d" % q, [128, 8, 128], BF16)
                    c.ip_ps = [a.ps("ip_psD%d_%d" % (i, q), [128, 512], F32) for i in range(2)]
                    c.tq_ps = a.ps("tq_psD%d" % q, [128, 8, 128], BF16)
                    c.tk_ps = a.ps("tk_psD%d" % q, [128, 8, 128], BF16)
                    dsets.append(c)
                for k in range(8):
                    for c4 in range(4):
                        load_weight_bf16(p, "sp", W1, w_in1, k, 1024, stg, vec_sb[:, 52 + k:53 + k],
                                         col0=c4 * 1024, dst_col0=c4 * 1024)
                p.dma("sp", posf[:, :], pos[:, :])
                invf = cst_sb[:, 448:456]
                for t in range(NT4):
                    p.ts("dve", ang[:, t, :], invf, posf[:, t:t + 1], None, ALU.mult)
                p.ts("pool", tm1[:, :, :], ang[:, :, :], 1.0 / (2 * math.pi), None, ALU.mult)
                p.ts("dve", tm2[:, :, :], tm1[:, :, :], 12582912.0, None, ALU.add)
                p.ts("pool", tm3[:, :, :], tm2[:, :, :], -12582912.0, None, ALU.add)
                p.tt("dve", ang[:, :, :], tm1[:, :, :], tm3[:, :, :], ALU.subtract)
                p.act(sinT[:, :, :], ang[:, :, :], AF.Sin, scale=2 * math.pi)
                p.act(tm2[:, :, :], ang[:, :, :], AF.Sin, scale=math.pi)
                p.tt("dve", tm3[:, :, :], tm2[:, :, :], tm2[:, :, :], ALU.mult)
                p.ts("pool", cosT[:, :, :], tm3[:, :, :], -2.0, 1.0, ALU.mult, ALU.add)

                def tileD(t, c):
                    ybt = R(ybT[:, :, t * 128:(t + 1) * 128], "ybT_%d" % t)
                    qk, sq, ssq, qkb, vt, gt = c.qk, c.sq, c.ssq, c.qkb, c.vt, c.gt
                    for cg in range(8):
                        ips = c.ip_ps[cg % 2]
                        for k in range(8):
                            p.mm(ips[:, :], ybt[:, k, :], W1[:, k, cg * 512:(cg + 1) * 512], k == 0, k == 7)
                        yield
                        if cg < 4:
                            p.cp("act", qk[:, cg * 512:(cg + 1) * 512], ips[:, :])
                        elif cg < 6:
                            p.cp("act", vt[:, (cg - 4) * 512:(cg - 3) * 512], ips[:, :])
                        else:
                            p.act(gt[:, (cg - 6) * 512:(cg - 5) * 512], ips[:, :], AF.Silu)
                        yield
                    p.act(sq[:, :], qk[:, :], AF.Square)
                    yield
                    p.red("dve", ssq[:, 0:32], sq[:, :].rearrange("p (a d) -> p a d", d=64))
                    yield
                    p.act(ssq[:, 32:64], ssq[:, 0:32], AF.Ln, bias=eps_t[:, 0:1], scale=1.0 / 64)
                    yield
                    p.ts("dve", ssq[:, 0:32], ssq[:, 32:64], -0.5, None, ALU.mult)
                    yield
                    p.act(ssq[:, 64:96], ssq[:, 0:32], AF.Exp)
                    yield
                    qk3 = qk[:, :].rearrange("p (a d) -> p a d", d=64)
                    p.tt("dve", qk3, qk3, bc(ssq[:, 64:96].unsqueeze(2), [128, 32, 64]), ALU.mult)
                    yield
                    p.tt("dve", qk3[:, 0:16, :], qk3[:, 0:16, :], bc(bcc[:, 0:64].unsqueeze(1), [128, 16, 64]), ALU.mult)
                    yield
                    p.tt("dve", qk3[:, 16:32, :], qk3[:, 16:32, :], bc(bcc[:, 64:128].unsqueeze(1), [128, 16, 64]), ALU.mult)
                    yield
                    p.cp("act", qkb[:, :], qk[:, :])
                    qb3 = qkb[:, :].rearrange("p (a d) -> p a d", d=64)
                    cb_ = bc(cosT[:, t, :].unsqueeze(1), [128, 32, 8])
                    sb_ = bc(sinT[:, t, :].unsqueeze(1), [128, 32, 8])
                    p.tt("dve", c.rA[:, :, :], qk3[:, :, 0:8], cb_, ALU.mult)
                    yield
                    p.tt("dve", c.rA2[:, :, :], qk3[:, :, 8:16], cb_, ALU.mult)
                    p.tt("pool", c.rB[:, :, :], qk3[:, :, 8:16], sb_, ALU.mult)
                    yield
                    p.tt("pool", c.rB2[:, :, :], qk3[:, :, 0:8], sb_, ALU.mult)
                    yield
                    p.tt("dve", qb3[:, :, 0:8], c.rA[:, :, :], c.rB[:, :, :], ALU.subtract)
                    yield
                    p.tt("dve", qb3[:, :, 8:16], c.rA2[:, :, :], c.rB2[:, :, :], ALU.add)
                    yield
                    for h in range(8):
                        p.tr(c.tq_ps[:, h, :], qkb[:, h * 128:(h + 1) * 128], ident[:, :])
                    yield
                    for h in range(8):
                        p.tr(c.tk_ps[:, h, :], qkb[:, 1024 + h * 128:1024 + (h + 1) * 128], ident[:, :])
                    yield
                    p.cp("act", c.QTt[:, :, :], c.tq_ps[:, :, :])
                    p.cp("dve", c.KTt[:, :, :], c.tk_ps[:, :, :])
                    yield
                    tc_ = slice(t * 128, (t + 1) * 128)
                    p.dma("sp", QTs[:, tc_].rearrange("(h p) t -> p h t", p=128), c.QTt[:, :, :])
                    for h in range(8):
                        p.dma("pool" if h % 2 else "sp", kT_in[h][:, tc_], c.KTt[:, h, :])
                    p.dma("sp", v_in[t // 4][(t % 4) * 128:(t % 4 + 1) * 128, :], vt[:, :])
                    p.dma("pool", gs[tc_, :], gt[:, :])
                    yield

                def laneD(q):
                    for t in range(q, NT4, 2):
                        yield from tileD(t, dsets[q])
                run_lanes([laneD(0), laneD(1)], 12)
                for h in range(8):
                    p.collective("AllGather", groups, kT_in[h], kT_all[h])
                for c_ in range(NVC):
                    p.collective("AllGather", groups, v_in[c_], v_all[c_])
                p.barrier()
                p.flush(final=(stop_after == "D"))
            if stop_after == "D":
                return nc
        with contextlib.ExitStack() as stO:
            ao = Alloc(nc, stO)
            NT4 = NTL * 4
            ogT = ao.sb("ogT", [128, NT4, 8, 128], BF16)
            with contextlib.ExitStack() as st:
                a = Alloc(nc, st)
                KT = a.sb("KT", [128, 4, NOWN], BF16)
                V = a.sb("V", [128, 4, NT4, 132], BF16)
                QTz = [a.sb("QTz%d" % m, [128, NOWN], BF16) for m in range(2)]
                msk = a.sb("msk", [128, 16, 512], BF16)
                NPT = 5
                LA = 3
                PT = [a.sb("PT%d" % i, [128, 1024], BF16) for i in range(NPT)]
                gh2 = [a.sb("gh2_%d" % i, [128, NT4, 128], BF16) for i in range(2)]
                Osb = [[a.sb("Osb%d_%d" % (b_, j), [128, 2, 132], F32) for j in range(4)] for b_ in range(2)]
                zr = a.sb("zr", [128, 4], F32)
                o0 = a.sb("o0", [128, 128], F32)
                o1 = a.sb("o1", [128, 128], F32)
                osq = a.sb("osq", [128, 128], F32)
                sso = a.sb("sso", [128, 4], F32)
                NSP = 2
                s_ps = [a.ps("s_ps%d" % i, [128, 1024], F32) for i in range(NSP)]
                O_ps = [[a.ps("O_ps%d_%d" % (m, j), [128, 2, 132], F32) for j in range(2)] for m in range(2)]
                p.dma("sp", msk[:, :, :], masks[:, :].rearrange("p (a b) -> p a b", a=16))
                p.memset("dve", QTz[0][64:128, :], 0.0)
                p.memset("pool", QTz[1][0:64, :], 0.0)
                for r in range(4):
                    for kt_ in range(NT4):
                        p.memset("dve" if kt_ % 2 == 0 else "pool", V[:, r, kt_, 128:132], 1.0)

                def epilogue(h, i, ob, ghh):
                    for qs in range(4):
                        t = i * 4 + qs
                        Oa = ob[qs // 2]
                        Ob = ob[2 + qs // 2]
                        j = qs % 2
                        p.recip("dve", zr[:, 0:1], Oa[:, j, 128:129])
                        p.recip("dve", zr[:, 1:2], Ob[:, j, 128:129])
                        yield
                        p.tt("pool", zr[:, 2:3], zr[:, 1:2], lamt[:, 3:4], ALU.mult)
                        p.act(o0[:, :], Oa[:, j, 0:128], AF.Copy, scale=zr[:, 0:1])
                        yield
                        p.act(o1[:, :], Ob[:, j, 0:128], AF.Copy, scale=zr[:, 2:3])
                        yield
                        p.tt("dve", o0[:, :], o0[:, :], o1[:, :], ALU.add)
                        yield
                        p.tt("pool", osq[:, :], o0[:, :], o0[:, :], ALU.mult)
                        yield
                        p.red("dve", sso[:, 0:1], osq[:, :])
                        yield
                        p.act(sso[:, 1:2], sso[:, 0:1], AF.Ln, bias=eps_t[:, 0:1], scale=1.0 / 128)
                        yield
                        p.ts("dve", sso[:, 3:4], sso[:, 1:2], -0.5, None, ALU.mult)
                        yield
                        p.act(sso[:, 2:3], sso[:, 3:4], AF.Exp)
                        yield
                        p.ts("dve", o1[:, :], o0[:, :], sso[:, 2:3], None, ALU.mult)
                        yield
                        p.tt("dve", o1[:, :], o1[:, :], bcc[:, 384:512], ALU.mult)
                        yield
                        p.tt("pool", ogT[:, t, h, :], o1[:, :], ghh[:, t, :], ALU.mult)
                        yield

                pending = []

                def advance(n):
                    for _ in range(n):
                        if not pending:
                            return
                        try:
                            next(pending[0])
                        except StopIteration:
                            pending.pop(0)

                kglob = 0
                blk = 0
                for h in range(8):
                    ghh = gh2[h % 2]
                    for r in range(4):
                        p.dma("sp", KT[:, r, :], kT_all[h][r * 128:(r + 1) * 128, :])
                        for c_ in range(NVC):
                            p.dma("pool" if c_ % 2 == 0 else "sp", V[:, r, c_ * 4:(c_ + 1) * 4, 0:128],
                                  v_all[c_][r * VR:(r + 1) * VR, h * 128:(h + 1) * 128].rearrange("(t p) d -> p t d", p=128))
                    p.dma("sp", QTz[0][0:64, :], QTs[h * 128:h * 128 + 64, :])
                    p.dma("pool", QTz[1][64:128, :], QTs[h * 128 + 64:(h + 1) * 128, :])
                    p.dma("pool", ghh[:, :, :], gs[:, h * 128:(h + 1) * 128].rearrange("(t p) d -> p t d", p=128))
                    for i in range(NTL):
                        keys = [(r, ip * 4 + uu, None) for r in range(4) for ip in range(i) for uu in range(4)]
                        keys += [(r, i * 4 + uu, r * 4 + uu) for r in range(4) for uu in range(4)]
                        tiles = [(m, idx, r, kt, mk) for idx, (r, kt, mk) in enumerate(keys) for m in range(2)]
                        n = len(tiles)
                        LA2 = 4
                        for k0 in range(0, n + LA2, 2):
                            if k0 < n:
                                pi = ((kglob + k0) // 2)
                                spair = s_ps[pi % NSP]
                                ptp = PT[pi % NPT]
                                m0, idx0, r, kt, mk = tiles[k0]
                                for hf in range(2):
                                    p.mm(spair[:, hf * 512:(hf + 1) * 512], KT[:, r, kt * 128:(kt + 1) * 128],
                                         QTz[hf][:, i * TILE:(i + 1) * TILE], True, True)
                                p.act(ptp[:, :], spair[:, :], AF.Exp, bias=negM[:, 0:1], scale=0.125)
                                if mk is not None:
                                    p.tt("dve", ptp[:, :].rearrange("p (a b) -> p a b", a=2),
                                         ptp[:, :].rearrange("p (a b) -> p a b", a=2),
                                         bc(msk[:, mk, :].unsqueeze(1), [128, 2, 512]), ALU.mult)
                            kp = k0 - LA2
                            if 0 <= kp < n:
                                pi = ((kglob + kp) // 2)
                                ptp = PT[pi % NPT]
                                m0, idx, r, kt, mk = tiles[kp]
                                for hf in range(2):
                                    for qs in range(4):
                                        p.mm(O_ps[hf][qs // 2][:, qs % 2, 0:129],
                                             ptp[:, hf * 512 + qs * 128:hf * 512 + (qs + 1) * 128],
                                             V[:, r, kt, 0:129], idx == 0 and qs % 2 == 0, idx == len(keys) - 1)
                            advance(4)
                        kglob += n
                        while pending:
                            advance(1)
                        ob = Osb[blk % 2]
                        blk += 1
                        p.cp("act", ob[0][:, :, :], O_ps[0][0][:, :, :])
                        p.cp("dve", ob[1][:, :, :], O_ps[0][1][:, :, :])
                        p.cp("act", ob[2][:, :, :], O_ps[1][0][:, :, :])
                        p.cp("dve", ob[3][:, :, :], O_ps[1][1][:, :, :])
                        pending.append(epilogue(h, i, ob, ghh))
                while pending:
                    advance(1)
                p.barrier()
                p.flush(final=(stop_after == "E1"))
            if stop_after == "E1":
                return nc
            with contextlib.ExitStack() as st:
                a = Alloc(nc, st)
                Wo1 = a.sb("Wo1", [128, 8, D], BF16)
                stg = [a.sb("stgF%d" % i_, [128, 1024], F32) for i_ in range(2)]
                x1t = [a.sb("x1tF%d" % i, [128, D], F32) for i in range(2)]
                ot = [a.sb("otF%d" % i, [128, D], F32) for i in range(2)]
                f_ps = [a.ps("f_ps%d" % i, [128, 512], F32) for i in range(2)]
                tf_ps = [a.ps("tf_ps%d" % i, [128, 8, 128], BF16) for i in range(2)]
                ogTt = [a.sb("ogTt%d" % i, [128, 8, 128], BF16) for i in range(2)]
                for k in range(8):
                    load_weight_bf16(p, "sp", Wo1, w_out1, k, 1024, stg, None)
                for t in range(NT4):
                    xx = x1t[t % 2]
                    oo = ot[t % 2]
                    p.dma("sp", xx[:, :], x1s[t * 128:(t + 1) * 128, :])
                    for h in range(8):
                        p.tr(tf_ps[t % 2][:, h, :], ogT[:, t, h, :], ident[:, :])
                    p.cp("act", ogTt[t % 2][:, :, :], tf_ps[t % 2][:, :, :])
                    for dh in range(2):
                        for h in range(8):
                            p.mm(f_ps[dh][:, :], ogTt[t % 2][:, h, :], Wo1[:, h, dh * 512:(dh + 1) * 512],
                                 h == 0, h == 7)
                        p.tt("dve", oo[:, dh * 512:(dh + 1) * 512], f_ps[dh][:, :], xx[:, dh * 512:(dh + 1) * 512], ALU.add)
                    p.dma("pool", out[t * 128:(t + 1) * 128, :], oo[:, :])
                p.flush(final=True)
    return nc


def _consts():
    c = np.zeros((128, 1024), np.float32)
    c[:, 0:128] = np.eye(128, dtype=np.float32)
    j = np.arange(64)[:, None]
    l = np.arange(64)[None, :]
    c[0:64, 128:192] = (j <= l)
    c[0:64, 192:320] = 1.0
    c[0:64, 320:384] = (j > l)
    c[0:64, 384:448] = (j <= l)
    c[:, 448:456] = (500000.0 ** (-2.0 * np.arange(8, dtype=np.float32) / 16.0)).astype(np.float32)[None, :]
    return c


def prep_inputs(inp, S):
    NT = S // TILE
    NTL = NT // 4
    x = np.asarray(inp["x"], np.float32)
    w_in0 = np.asarray(inp["a_w_in"], np.float32)[0]
    conv_w = np.asarray(inp["a_conv_w"], np.float32)[0]
    conv_b = np.asarray(inp["a_conv_b"], np.float32)[0]
    maps = []
    cst = _consts()
    for c in range(8):
        b, s = c // 4, c % 4
        g = s
        m = {}
        m["xb"] = np.ascontiguousarray(x[b, :S])
        xo = np.zeros((NTL, TILE + HALO, D), np.float32)
        for i in range(NTL):
            jt = 4 * i + s
            lo = jt * TILE - HALO
            if lo < 0:
                xo[i, HALO:] = x[b, 0:TILE]
            else:
                xo[i] = x[b, lo:lo + TILE + HALO]
        m["xo"] = xo
        posb = np.asarray(inp["positions"])[b]
        own = np.concatenate([posb[(4 * i + s) * TILE:(4 * i + s + 1) * TILE] for i in range(NTL)])
        m["pos"] = np.ascontiguousarray(own.reshape(NTL * 4, 128).T.astype(np.float32))
        m["cst"] = cst
        cols = np.concatenate([np.arange(g * 256, (g + 1) * 256),
                               1024 + np.arange(g * 256, (g + 1) * 256),
                               2048 + np.arange(g * 128, (g + 1) * 128),
                               2560 + np.arange(g * 128, (g + 1) * 128),
                               3072 + np.arange(g * 4, (g + 1) * 4)])
        m["w_ssd"] = np.ascontiguousarray(w_in0[:, cols])
        va = np.zeros((128, 64), np.float32)
        va[:, 0:8] = np.asarray(inp["a_norm_w"], np.float32)[0].reshape(8, 128).T
        ccols = [np.arange(g * 256, g * 256 + 128), np.arange(g * 256 + 128, (g + 1) * 256),
                 1024 + np.arange(g * 128, (g + 1) * 128), 1536 + np.arange(g * 128, (g + 1) * 128)]
        for ct in range(4):
            va[:, 8 + ct * 4:12 + ct * 4] = conv_w[:, ccols[ct]].T
            va[:, 24 + ct] = conv_b[ccols[ct]]
        va[:, 28:36] = np.asarray(inp["a_dw_b"], np.float32)[0].reshape(8, 128).T
        va[:, 36:44] = np.asarray(inp["a_ln_w"], np.float32)[0].reshape(8, 128).T
        va[:, 44:52] = np.asarray(inp["a_ln_b"], np.float32)[0].reshape(8, 128).T
        va[:, 52:60] = np.asarray(inp["c_norm_w"], np.float32)[0].reshape(8, 128).T
        m["vec_a"] = va
        ba = np.zeros((64, 512), np.float32)
        ba[:, 0:4] = np.asarray(inp["a_dt_bias"], np.float32)[0][g * 4:(g + 1) * 4][None, :]
        ba[:, 4:8] = np.asarray(inp["a_a_log"], np.float32)[0][g * 4:(g + 1) * 4][None, :]
        ba[:, 8:12] = np.asarray(inp["a_d_skip"], np.float32)[0][g * 4:(g + 1) * 4][None, :]
        ba[:, 256:512] = np.asarray(inp["a_ssd_norm_w"], np.float32)[0][g * 256:(g + 1) * 256][None, :]
        m["bc_a"] = ba
        m["w_conf"] = np.ascontiguousarray(w_in0[:, 3088:6160])
        dww = np.asarray(inp["a_dw_w"], np.float32)[0]
        m["dw_w"] = np.ascontiguousarray(dww.T.reshape(8, 128, 31).transpose(1, 0, 2).reshape(128, 248))
        m["w_out0"] = np.asarray(inp["a_w_out"], np.float32)[0]
        m["w_in1"] = np.asarray(inp["c_w_in"], np.float32)[0]
        bcv = np.concatenate([np.asarray(inp[n], np.float32)[0] for n in
                              ("c_q_norm_w", "c_k_norm_w", "c_lq1", "c_lk1", "c_lq2", "c_lk2", "c_subln_w")])
        m["bc_c"] = np.ascontiguousarray(np.broadcast_to(bcv[None, :], (128, 512)))
        m["w_out1"] = np.asarray(inp["c_w_out"], np.float32)[0]
        mk = np.zeros((128, 16, 512), np.float32)
        kk = np.arange(128)[:, None] // 64
        qq = np.arange(512)[None, :] // 64
        for r in range(4):
            for uu in range(4):
                mk[:, r * 4 + uu, :] = (r * 8 + 2 * uu + kk <= s * 8 + qq)
        m["masks"] = np.ascontiguousarray(mk.reshape(128, 16 * 512).astype(ml_dtypes.bfloat16))
        fl = np.zeros((128, 8), np.float32)
        fl[:, s] = 1.0
        m["flags"] = fl
        maps.append(m)
    return maps


def kernel(**inputs):
    S = inputs["x"].shape[1]
    nc = build(S)
    maps = prep_inputs(inputs, S)
    res = run_bass_kernel_spmd(nc, maps, core_ids=list(range(8)))
    NT = S // TILE
    NTL = NT // 4
    outp = np.zeros((2, S, D), np.float32)
    for c in range(8):
        b, s = c // 4, c % 4
        o = res.results[c]["out"]
        for i in range(NTL):
            jt = 4 * i + s
            outp[b, jt * TILE:(jt + 1) * TILE] = o[i * TILE:(i + 1) * TILE]
    return outp
```

```python
import contextlib
import math
import numpy as np
import ml_dtypes
import concourse.bass as bass
import concourse.mybir as mybir
from concourse.bass_utils import run_bass_kernel_spmd

F32 = mybir.dt.float32
BF16 = mybir.dt.bfloat16
I32 = mybir.dt.int32
AF = mybir.ActivationFunctionType
ALU = mybir.AluOpType
AX = mybir.AxisListType

D = 1024
EPS = 1e-6
LN_EPS = 1e-5
TILE = 512
HALO = 32
NB_DMA = 12
EPOCH = 30000


class Prog:
    ENGS = ("pe", "act", "dve", "pool", "sp")

    def __init__(self, nc, stack):
        self.nc = nc
        self.ops = {e: [] for e in self.ENGS}
        self.count = {e: 0 for e in self.ENGS}
        self.last_w = {}
        self.readers = {}
        self.seen = {e: {} for e in self.ENGS}
        self.sems = {}
        self.stack = stack
        self.dma_tot = {}
        self.dma_rr = {e: 0 for e in self.ENGS}
        self.cc_count = 0
        self.nops = 0

    def sem(self, key):
        if key not in self.sems:
            self.sems[key] = self.stack.enter_context(self.nc.semaphore("s_%s_%s" % key))
        return self.sems[key]

    def _deps(self, eng, reads, writes):
        w = {}

        def add(d):
            for k, v in d.items():
                if w.get(k, 0) < v:
                    w[k] = v
        for t in reads:
            add(self.last_w.get(t, {}))
        for t in writes:
            add(self.last_w.get(t, {}))
            add(self.readers.get(t, {}))
        out = []
        for k, v in w.items():
            if eng == "pe" and k[0] == "pe":
                continue
            if self.seen[eng].get(k, 0) >= v:
                continue
            self.seen[eng][k] = v
            out.append((k, v))
        return out

    def _record(self, ev, reads, writes):
        k, v = ev
        for t in reads:
            d = self.readers.setdefault(t, {})
            d[k] = max(d.get(k, 0), v)
        for t in writes:
            d = self.last_w.setdefault(t, {})
            d[k] = max(d.get(k, 0), v)

    def op(self, eng, fn, reads, writes):
        reads = [r.name if hasattr(r, "name") else r for r in reads]
        writes = [r.name if hasattr(r, "name") else r for r in writes]
        waits = self._deps(eng, reads, writes)
        c = self.count[eng]
        self.count[eng] += 1
        key = (eng, c // EPOCH)
        ev = (key, c % EPOCH + 1)
        self.sem(key)
        self.seen[eng][key] = ev[1]
        self.ops[eng].append((waits, fn, key, 1))
        self._record(ev, reads, writes)
        self.nops += 1

    def dma(self, q, out, in_, extra_reads=(), extra_writes=()):
        reads = [in_.name] + list(extra_reads)
        writes = [out.name] + list(extra_writes)
        waits = self._deps(q, reads, writes)
        slot = q + str(self.dma_rr[q])
        self.dma_rr[q] = (self.dma_rr[q] + 1) % NB_DMA
        key = ("dma", slot)
        self.sem(key)
        prev = self.dma_tot.get(slot, 0)
        if prev > 0 and self.seen[q].get(key, 0) < prev:
            waits.append((key, prev))
            self.seen[q][key] = prev
        self.dma_tot[slot] = prev + 16
        ev = (key, prev + 16)
        self.ops[q].append((waits, lambda e, o=out, i=in_: e.dma_start(out=o, in_=i), key, 16))
        self._record(ev, reads, writes)
        self.nops += 1

    def collective(self, kind, groups, in_t, out_t):
        q = "pool"
        reads = [in_t.name]
        writes = [out_t.name]
        waits = self._deps(q, reads, writes)
        key = ("cc", 0)
        self.sem(key)
        self.cc_count += 1
        ev = (key, self.cc_count)

        def fn(e):
            return e.collective_compute(kind, ALU.bypass, replica_groups=groups,
                                        ins=[in_t.ap().opt()], outs=[out_t.ap().opt()])
        self.ops[q].append((waits, fn, key, 1))
        self._record(ev, reads, writes)

    def barrier(self, with_cc=False):
        allev = {}
        for e in self.ENGS:
            c = self.count[e]
            if c > 0:
                allev[(e, (c - 1) // EPOCH)] = (c - 1) % EPOCH + 1
        for s_, v_ in self.dma_tot.items():
            if v_ > 0:
                allev[("dma", s_)] = v_
        if self.cc_count and with_cc:
            allev[("cc", 0)] = self.cc_count
        for e in self.ENGS:
            waits = []
            for k, v in allev.items():
                if k[0] == e:
                    continue
                if self.seen[e].get(k, 0) >= v:
                    continue
                self.seen[e][k] = v
                waits.append((k, v))
            if waits:
                self.ops[e].append((waits, None, None, 0))

    def flush(self, final=False):
        nc = self.nc
        if final:
            self.barrier(with_cc=True)
        ops = self.ops
        sems = self.sems
        with nc.Block() as block:
            def emit(eng_handle, lst):
                for waits, fn, key, amt in lst:
                    for k, v in waits:
                        eng_handle.wait_ge(sems[k], v)
                    if fn is not None:
                        ins = fn(eng_handle)
                        ins.then_inc(sems[key], amt)

            @block.tensor
            def _(e):
                emit(e, ops["pe"])

            @block.scalar
            def _(e):
                emit(e, ops["act"])

            @block.vector
            def _(e):
                emit(e, ops["dve"])

            @block.gpsimd
            def _(e):
                emit(e, ops["pool"])

            @block.sync
            def _(e):
                emit(e, ops["sp"])
        self.ops = {e: [] for e in self.ENGS}

    def mm(self, out, lhsT, rhs, start, stop, skip=False):
        rd, wr = [lhsT, rhs], [out]
        out, lhsT, rhs = U(out), U(lhsT), U(rhs)
        if rd is not None:
            self.op("pe", lambda e: e.matmul(out, lhsT=lhsT, rhs=rhs, start=start, stop=stop), rd, wr)
            return
        if skip:
            self.op("pe", lambda e: e.matmul(out, lhsT=lhsT, rhs=rhs, start=start, stop=stop,
                                             skip_group_check=True), [lhsT, rhs], [out])
        else:
            self.op("pe", lambda e: e.matmul(out, lhsT=lhsT, rhs=rhs, start=start, stop=stop),
                    [lhsT, rhs], [out])

    def tr(self, out, in_, ident):
        rd, wr = [in_, ident], [out]
        out, in_, ident = U(out), U(in_), U(ident)
        self.op("pe", lambda e: e.transpose(out=out, in_=in_, identity=ident), rd, wr)

    def act(self, out, in_, func, bias=None, scale=None, extra=()):
        kw = {}
        rd = [in_] + list(extra)
        wr = [out]
        if bias is not None:
            kw["bias"] = U(bias)
            if hasattr(bias, "name"):
                rd.append(bias)
        if scale is not None:
            kw["scale"] = U(scale)
            if hasattr(scale, "name"):
                rd.append(scale)
        out, in_ = U(out), U(in_)
        self.op("act", lambda e: e.activation(out=out, in_=in_, func=func, **kw), rd, wr)

    def tt(self, eng, out, in0, in1, op):
        rd, wr = [in0, in1], [out]
        out, in0, in1 = U(out), U(in0), U(in1)
        self.op(eng, lambda e: e.tensor_tensor(out=out, in0=in0, in1=in1, op=op), rd, wr)

    def ts(self, eng, out, in0, s1, s2, op0, op1=None):
        rd = [in0]
        wr = [out]
        for s in (s1, s2):
            if hasattr(s, "name"):
                rd.append(s)
        out, in0, s1, s2 = U(out), U(in0), U(s1), U(s2)
        if op1 is None:
            self.op(eng, lambda e: e.tensor_scalar(out=out, in0=in0, scalar1=s1, scalar2=None, op0=op0),
                    rd, wr)
        else:
            self.op(eng, lambda e: e.tensor_scalar(out=out, in0=in0, scalar1=s1, scalar2=s2, op0=op0,
                                                   op1=op1), rd, wr)

    def stt(self, eng, out, in0, scalar, in1, op0, op1):
        rd = [in0, in1]
        wr = [out]
        if hasattr(scalar, "name"):
            rd.append(scalar)
        out, in0, scalar, in1 = U(out), U(in0), U(scalar), U(in1)
        self.op(eng, lambda e: e.scalar_tensor_tensor(out=out, in0=in0, scalar=scalar, in1=in1,
                                                      op0=op0, op1=op1), rd, wr)

    def cp(self, eng, out, in_):
        rd, wr = [in_], [out]
        out, in_ = U(out), U(in_)
        if eng == "act":
            self.op(eng, lambda e: e.copy(out=out, in_=in_), rd, wr)
            return
        if eng != "act":
            self.op(eng, lambda e: e.tensor_copy(out=out, in_=in_), rd, wr)
            return
        if eng == "act":
            self.op(eng, lambda e: e.copy(out=out, in_=in_), [in_], [out])
        else:
            self.op(eng, lambda e: e.tensor_copy(out=out, in_=in_), [in_], [out])

    def red(self, eng, out, in_, op=None):
        rd, wr = [in_], [out]
        out, in_ = U(out), U(in_)
        self.op(eng, lambda e: e.tensor_reduce(out=out, in_=in_, axis=AX.X, op=op or ALU.add), rd, wr)

    def recip(self, eng, out, in_):
        rd, wr = [in_], [out]
        out, in_ = U(out), U(in_)
        self.op(eng, lambda e: e.reciprocal(out=out, in_=in_), rd, wr)

    def memset(self, eng, out, val):
        self.op(eng, lambda e: e.memset(out, val), [], [out])


class R:
    def __init__(self, ap, tok):
        self._ap = ap
        self.name = tok

    def __getitem__(self, k):
        return R(self._ap[k], self.name)

    def rearrange(self, *a, **k):
        return R(self._ap.rearrange(*a, **k), self.name)

    def unsqueeze(self, *a):
        return R(self._ap.unsqueeze(*a), self.name)

    def to_broadcast(self, *a):
        return R(self._ap.to_broadcast(*a), self.name)


def U(x):
    return x._ap if isinstance(x, R) else x


class Alloc:
    def __init__(self, nc, stack):
        self.nc = nc
        self.stack = stack
        self.n = 0

    def sb(self, name, shape, dt):
        return self.stack.enter_context(self.nc.sbuf_tensor(name, list(shape), dt))

    def ps(self, name, shape, dt):
        return self.stack.enter_context(self.nc.psum_tensor(name, list(shape), dt))


def bc(ap, shape):
    return ap.to_broadcast(list(shape))


def front_end_g(p, xt, junk, ss, xn, tp_ps, hnT, col0, ident, ntok=128, cp_eng="act", lnexp=False):
    p.tt("dve", junk[0:ntok, :], xt[0:ntok, :], xt[0:ntok, :], ALU.mult)
    yield
    p.red("dve", ss[0:ntok, 0:1], junk[0:ntok, :])
    yield
    if lnexp:
        p.act(ss[0:ntok, 1:2], ss[0:ntok, 0:1], AF.Ln, bias=EPS_AP[0][0:ntok, :], scale=1.0 / D)
        yield
        p.ts("dve", ss[0:ntok, 3:4], ss[0:ntok, 1:2], -0.5, None, ALU.mult)
        yield
        p.act(ss[0:ntok, 2:3], ss[0:ntok, 3:4], AF.Exp)
        yield
        p.ts("dve", xn[0:ntok, :], xt[0:ntok, :], ss[0:ntok, 2:3], None, ALU.mult)
        yield
    else:
        p.act(ss[0:ntok, 1:2], ss[0:ntok, 0:1], AF.Sqrt, bias=EPS_AP[0][0:ntok, :], scale=1.0 / D)
        yield
        p.recip("dve", ss[0:ntok, 2:3], ss[0:ntok, 1:2])
        yield
        p.act(xn[0:ntok, :], xt[0:ntok, :], AF.Copy, scale=ss[0:ntok, 2:3])
        yield
    for k in range(8):
        p.tr(tp_ps[:, k, 0:ntok], xn[0:ntok, k * 128:(k + 1) * 128], ident[0:ntok, 0:ntok])
    yield
    p.cp(cp_eng, hnT[:, :, col0:col0 + ntok], tp_ps[:, :, 0:ntok])
    yield


def front_end(*a_, **k_):
    for _ in front_end_g(*a_, **k_):
        pass


def run_lanes(lanes, lag):
    alive = [True] * len(lanes)

    def step(q):
        if alive[q]:
            try:
                next(lanes[q])
            except StopIteration:
                alive[q] = False
    for _ in range(lag):
        step(0)
    while any(alive):
        for q in range(len(lanes)):
            step(q)


EPS_AP = [None]


LW_CNT = [0]


def load_weight_bf16(p, q, dst, w_dram, k, ncols, stg, scale_ap, col0=0, dst_col0=0):
    n = LW_CNT[0]
    LW_CNT[0] += 1
    if isinstance(stg, (list, tuple)):
        stg = stg[n % len(stg)]
    q = "sp" if n % 2 == 0 else "pool"
    p.dma(q, stg[:, 0:ncols], w_dram[k * 128:(k + 1) * 128, col0:col0 + ncols])
    if scale_ap is not None:
        if n % 2 == 0:
            p.act(dst[:, k, dst_col0:dst_col0 + ncols], stg[:, 0:ncols], AF.Copy, scale=scale_ap)
        else:
            p.ts("dve", dst[:, k, dst_col0:dst_col0 + ncols], stg[:, 0:ncols], scale_ap, None, ALU.mult)
    else:
        p.cp("act" if n % 2 == 0 else "dve", dst[:, k, dst_col0:dst_col0 + ncols], stg[:, 0:ncols])


def build(S, stop_after="E", debug=False):
    NT = S // TILE
    NTL = NT // 4
    NOWN = NTL * TILE
    nc = bass.Bass("TRN2", target_bir_lowering=False)
    dbg_kind = "ExternalOutput" if debug else "Internal"

    def din(name, shape, dt=F32):
        return nc.dram_tensor(name, list(shape), dt, kind="ExternalInput")

    xb = din("xb", [S, D])
    xo = din("xo", [NTL, TILE + HALO, D])
    pos = din("pos", [128, NTL * 4], F32)
    cst = din("cst", [128, 1024])
    w_ssd = din("w_ssd", [D, 772])
    vec_a = din("vec_a", [128, 64])
    bc_a = din("bc_a", [64, 512])
    w_conf = din("w_conf", [D, 3072])
    dw_w = din("dw_w", [128, 8 * 31])
    w_out0 = din("w_out0", [2048, D])
    w_in1 = din("w_in1", [D, 4096])
    bc_c = din("bc_c", [128, 512])
    w_out1 = din("w_out1", [D, D])
    masks = din("masks", [128, 16 * 512], BF16)
    flags = din("flags", [128, 8])
    out = nc.dram_tensor("out", [NOWN, D], F32, kind="ExternalOutput")

    CW = min(S, 2048)
    NCA = S // CW
    yaT_in = [nc.dram_tensor("yaT_in%d" % i, [256, CW], BF16) for i in range(NCA)]
    yaT_all = [nc.dram_tensor("yaT_all%d" % i, [1024, CW], BF16) for i in range(NCA)]
    x1s = nc.dram_tensor("x1s", [NOWN, D], F32, kind=dbg_kind) if debug else nc.dram_tensor("x1s", [NOWN, D], F32)
    QTs = nc.dram_tensor("QTs", [1024, NOWN], BF16)
    gs = nc.dram_tensor("gs", [NOWN, 1024], BF16)
    kT_in = [nc.dram_tensor("kT_in%d" % i, [128, NOWN], BF16) for i in range(8)]
    kT_all = [nc.dram_tensor("kT_all%d" % i, [512, NOWN], BF16) for i in range(8)]
    VR = min(NOWN, 512)
    NVC = NOWN // VR
    v_in = [nc.dram_tensor("v_in%d" % i, [VR, 1024], BF16) for i in range(NVC)]
    v_all = [nc.dram_tensor("v_all%d" % i, [4 * VR, 1024], BF16) for i in range(NVC)]
    dbg = {}
    if debug:
        dbg["ya"] = nc.dram_tensor("dbg_ya", [1024, S], BF16, kind="ExternalOutput")
        dbg["yb"] = nc.dram_tensor("dbg_yb", [1024, NOWN], BF16, kind="ExternalOutput")

    groups = [[0, 1, 2, 3], [4, 5, 6, 7]]

    def dump(p, name, ap, shape, dt=F32):
        if not debug:
            return
        t = nc.dram_tensor("dd_" + name, list(shape), dt, kind="ExternalOutput")
        p.dma("pool", t[tuple(slice(0, n) for n in shape)], ap)

    with contextlib.ExitStack() as gstack:
        p = Prog(nc, gstack)
        ga = Alloc(nc, gstack)
        cst_sb = ga.sb("cst_sb", [128, 1024], F32)
        ident = ga.sb("ident", [128, 128], BF16)
        eps_t = ga.sb("eps_t", [128, 4], F32)
        vec_sb = ga.sb("vec_sb", [128, 64], F32)
        p.dma("sp", cst_sb[:, :], cst[:, :])
        p.dma("sp", vec_sb[:, :], vec_a[:, :])
        p.cp("dve", ident[:, :], cst_sb[:, 0:128])
        p.memset("dve", eps_t[:, 0:1], EPS)
        p.memset("dve", eps_t[:, 1:2], LN_EPS)
        p.memset("dve", eps_t[:, 2:3], 1.0)
        EPS_AP[0] = eps_t[:, 0:1]
        bcc = ga.sb("bcc", [128, 512], F32)
        lamt = ga.sb("lamt", [128, 8], F32)
        negM = ga.sb("negM", [128, 4], F32)
        ltmp = ga.sb("ltmp", [128, 64], F32)
        p.dma("sp", bcc[:, :], bc_c[:, :])
        LAM_INIT = 0.8 - 0.6 * math.exp(-0.3 * 1)
        p.tt("dve", ltmp[:, :], bcc[:, 128:192], bcc[:, 192:256], ALU.mult)
        p.red("dve", lamt[:, 0:1], ltmp[:, :])
        p.tt("dve", ltmp[:, :], bcc[:, 256:320], bcc[:, 320:384], ALU.mult)
        p.red("dve", lamt[:, 1:2], ltmp[:, :])
        p.act(lamt[:, 4:6], lamt[:, 0:2], AF.Exp)
        p.tt("dve", lamt[:, 2:3], lamt[:, 4:5], lamt[:, 5:6], ALU.subtract)
        p.ts("pool", lamt[:, 3:4], lamt[:, 2:3], LAM_INIT, -1.0, ALU.add, ALU.mult)
        p.ts("dve", bcc[:, 384:512], bcc[:, 384:512], 1.0 - LAM_INIT, None, ALU.mult)
        p.tt("dve", ltmp[:, :], bcc[:, 0:64], bcc[:, 0:64], ALU.mult)
        p.red("dve", negM[:, 1:2], ltmp[:, :], ALU.max)
        p.tt("dve", ltmp[:, :], bcc[:, 64:128], bcc[:, 64:128], ALU.mult)
        p.red("dve", negM[:, 2:3], ltmp[:, :], ALU.max)
        p.tt("pool", negM[:, 3:4], negM[:, 1:2], negM[:, 2:3], ALU.mult)
        p.ts("dve", ltmp[:, 0:1], negM[:, 3:4], 64.0, None, ALU.mult)
        p.act(ltmp[:, 1:2], ltmp[:, 0:1], AF.Sqrt)
        p.ts("dve", negM[:, 0:1], ltmp[:, 1:2], -1.0, None, ALU.mult)
        tri = cst_sb[0:64, 128:192]
        ones64 = cst_sb[0:64, 192:320]
        maskSU = cst_sb[0:64, 320:384]
        maskLE = cst_sb[0:64, 384:448]

        with contextlib.ExitStack() as st:
            a = Alloc(nc, st)
            Wssd = a.sb("Wssd", [128, 8, 772], BF16)
            stg = [a.sb("stgA%d" % i_, [128, 772], F32) for i_ in range(2)]
            bca = a.sb("bca", [64, 512], F32)
            xt = [a.sb("xtA%d" % i, [128, D], F32) for i in range(2)]
            xn = a.sb("xnA", [128, D], BF16)
            junk = a.sb("junkA", [128, D], F32)
            ss = a.sb("ssA", [128, 4], F32)
            hnT2 = [a.sb("hnTA%d" % i, [128, 8, TILE], BF16) for i in range(2)]
            pre = [a.sb("preA%d" % i, [128, TILE + 3], F32) for i in range(4)]
            acc = a.sb("accA", [128, TILE], F32)
            xbc2 = [[a.sb("xbcA%d_%d" % (j, i), [128, TILE], BF16) for i in range(4)] for j in range(2)]
            yaT2 = [a.sb("yaTA%d" % i, [128, 2, TILE], BF16) for i in range(2)]
            Sst = a.sb("SstA", [128, 256], F32)
            Sbf = a.sb("SbfA", [128, 256], BF16)

            class CS:
                pass
            sets = []
            tp_ps = a.ps("tp_psA", [128, 8, 128], BF16)
            bk1 = a.ps("bk1A", [128, 1024], BF16)
            bk6 = a.ps("bk6A", [128, 512], F32)
            mm_ps = [a.ps("mm_psA0", [128, TILE], F32)]
            for q in range(2):
                c = CS()
                n_ = lambda nm: "%sA_%d" % (nm, q)
                c.sz = a.sb(n_("sz"), [64, 256], F32)
                c.sze = a.sb(n_("sze"), [64, 256], F32)
                c.szr = a.sb(n_("szr"), [64, 256], F32)
                c.dtt = a.sb(n_("dtt"), [64, 32], F32)
                c.dst = a.sb(n_("dst"), [64, 8], F32)
                c.cdb = a.sb(n_("cdb"), [128, 4], F32)
                c.xtok = a.sb(n_("xtok"), [64, 256], F32)
                c.btok = a.sb(n_("btok"), [64, 128], BF16)
                c.xd = a.sb(n_("xd"), [64, 256], BF16)
                c.xdd = a.sb(n_("xdd"), [64, 256], BF16)
                c.xdf = a.sb(n_("xdf"), [64, 256], F32)
                c.cbm = a.sb(n_("cbm"), [64, 64], F32)
                c.Lh = [a.sb(n_("Lh%d" % i), [64, 64], F32) for i in range(2)]
                c.dec = [a.sb(n_("dec%d" % i), [64, 64], F32) for i in range(2)]
                c.Mh = [a.sb(n_("Mh%d" % i), [64, 64], BF16) for i in range(2)]
                c.y1 = a.sb(n_("y1"), [64, 256], F32)
                c.y3 = a.sb(n_("y3"), [64, 256], F32)
                c.ssy = a.sb(n_("ssy"), [64, 4], F32)
                c.y6 = a.sb(n_("y6"), [64, 256], BF16)
                bkA = a.ps(n_("bkzy"), [128, 512], F32)
                bkB = a.ps(n_("bkos"), [128, 512], F32)
                c.z_ps = bkA[0:64, 0:256]
                c.yd_ps = bkA[0:64, 256:512]
                c.yo_ps = bkB[0:64, 0:256]
                c.st_ps = bkB[:, 256:512]
                small = bk6[:, q * 256:q * 256 + 256]
                c.sm_ps = small[:, 0:16]
                c.cb_ps = small[0:64, 16:80]
                c.seg_ps = [small[0:64, 80:144], small[0:64, 144:208]]
                half = bk1[:, q * 512:(q + 1) * 512]
                c.xt_ps = half[0:64, 0:384]
                c.yT_ps = half[:, 384:512].rearrange("p (a b) -> p a b", a=2)
                sets.append(c)

            for k in range(8):
                load_weight_bf16(p, "sp", Wssd, w_ssd, k, 772, stg, vec_sb[:, k:k + 1])
            p.dma("sp", bca[:, :], bc_a[:, :])
            p.act(bca[:, 16:20], bca[:, 4:8], AF.Exp)
            p.ts("dve", bca[:, 4:8], bca[:, 16:20], -1.0, None, ALU.mult)
            p.memset("dve", Sst[:, :], 0.0)
            p.memset("dve", Sbf[:, :], 0.0)
            for i in range(4):
                p.memset("dve", pre[i][:, 0:3], 0.0)

            def prologue(T):
                hnT = hnT2[T % 2]
                xbc = xbc2[T % 2]
                for u in range(4):
                    x_t = xt[u % 2]
                    p.dma("sp", x_t[:, :], xb[T * TILE + u * 128:T * TILE + (u + 1) * 128, :])
                    yield from front_end_g(p, x_t, junk, ss, xn, tp_ps, hnT, u * 128, ident, lnexp=True)
                for ct in range(4):
                    mps = mm_ps[0]
                    for k in range(8):
                        p.mm(mps[:, :], Wssd[:, k, 256 + ct * 128:256 + (ct + 1) * 128], hnT[:, k, :],
                             k == 0, k == 7)
                    yield
                    p.cp("act", pre[ct][:, 3:TILE + 3], mps[:, :])
                    yield
                    cw = 8 + ct * 4
                    p.ts("dve", acc[:, :], pre[ct][:, 0:TILE], vec_sb[:, cw:cw + 1], None, ALU.mult)
                    yield
                    for j in range(1, 4):
                        p.stt("dve", acc[:, :], pre[ct][:, j:j + TILE], vec_sb[:, cw + j:cw + j + 1],
                              acc[:, :], ALU.mult, ALU.add)
                        yield
                    p.act(xbc[ct][:, :], acc[:, :], AF.Silu, bias=vec_sb[:, 24 + ct:25 + ct])
                    p.cp("dve", pre[ct][:, 0:3], pre[ct][:, TILE:TILE + 3])
                    yield

            pro = {}

            def pro_start(T):
                if T < NT and T not in pro:
                    pro[T] = prologue(T)

            def pro_step(T, n=1):
                g = pro.get(T)
                for _ in range(n):
                    if g is None:
                        return
                    try:
                        next(g)
                    except StopIteration:
                        pro[T] = None
                        return

            def pro_finish(T):
                pro_start(T)
                while pro.get(T) is not None:
                    pro_step(T)

            def chunk(cg, c):
                T, ch = cg // 8, cg % 8
                hnT = hnT2[T % 2]
                xbc = xbc2[T % 2]
                yaT = yaT2[T % 2]
                t0 = ch * 64
                tk = slice(t0, t0 + 64)
                dtt, dst, cdb = c.dtt, c.dst, c.cdb
                for k in range(8):
                    p.mm(c.z_ps[:, :], hnT[:, k, tk], Wssd[:, k, 0:256], k == 0, k == 7)
                for k in range(8):
                    p.mm(c.sm_ps[0:64, 0:4], hnT[:, k, tk], Wssd[:, k, 768:772], k == 0, k == 7)
                yield 'a'
                p.act(c.sze[:, :], c.z_ps[:, :], AF.Exp, scale=-1.0)
                p.tt("dve", dtt[:, 16:20], c.sm_ps[0:64, 0:4], bca[:, 0:4], ALU.add)
                yield 'a'
                p.ts("dve", c.sze[:, :], c.sze[:, :], 1.0, None, ALU.add)
                p.act(dtt[:, 20:24], dtt[:, 16:20], AF.Exp)
                yield 'a'
                p.recip("dve", c.szr[:, :], c.sze[:, :])
                yield 'a'
                p.ts("dve", dtt[:, 24:28], dtt[:, 20:24], 1.0, None, ALU.add)
                yield 'a'
                p.act(dtt[:, 0:4], dtt[:, 24:28], AF.Ln)
                p.tt("dve", c.sz[:, :], c.z_ps[:, :], c.szr[:, :], ALU.mult)
                yield 'a'
                p.tt("pool", dtt[:, 4:8], dtt[:, 0:4], bca[:, 4:8], ALU.mult)
                yield 'a'
                p.mm(c.sm_ps[0:64, 4:8], tri, dtt[:, 4:8], True, True)
                p.mm(c.sm_ps[0:64, 8:12], ones64[:, 0:64], dtt[:, 4:8], True, True)
                p.mm(c.sm_ps[:, 12:16], ones64, dtt[:, 4:8], True, True)
                p.tr(c.xt_ps[:, 0:128], xbc[0][:, tk], ident[:, :])
                p.tr(c.xt_ps[:, 128:256], xbc[1][:, tk], ident[:, :])
                p.tr(c.xt_ps[:, 256:384], xbc[2][:, tk], ident[:, :])
                p.mm(c.cb_ps[:, :], xbc[2][:, tk], xbc[3][:, tk], True, True)
                yield 'a'
                p.cp("act", dtt[:, 8:12], c.sm_ps[0:64, 4:8])
                p.act(dtt[:, 12:16], c.sm_ps[0:64, 4:8], AF.Exp)
                yield 'a'
                p.tt("dve", dst[:, 0:4], c.sm_ps[0:64, 8:12], dtt[:, 8:12], ALU.subtract)
                p.act(cdb[:, :], c.sm_ps[:, 12:16], AF.Exp)
                yield 'a'
                p.act(dst[:, 4:8], dst[:, 0:4], AF.Exp)
                p.tt("dve", c.cbm[:, :], c.cb_ps[:, :], maskLE, ALU.mult)
                yield 'a'
                p.cp("act", c.xtok[:, :], c.xt_ps[:, 0:256])
                yield 'a'
                p.cp("act", c.btok[:, :], c.xt_ps[:, 256:384])
                x3 = c.xtok[:, :].rearrange("p (h d) -> p h d", h=4)
                p.tt("dve", c.xdf[:, :].rearrange("p (h d) -> p h d", h=4), x3,
                     bc(dtt[:, 0:4].unsqueeze(2), [64, 4, 64]), ALU.mult)
                yield 'a'
                p.cp("dve", c.xd[:, :], c.xdf[:, :])
                p.tt("dve", c.xdd[:, :].rearrange("p (h d) -> p h d", h=4),
                     c.xdf[:, :].rearrange("p (h d) -> p h d", h=4),
                     bc(dst[:, 4:8].unsqueeze(2), [64, 4, 64]), ALU.mult)
                yield 1
                p.mm(c.yo_ps[:, :], xbc[3][:, tk], Sbf[:, :], True, True)
                p.mm(c.st_ps[:, :], c.btok[:, :], c.xdd[:, :], True, True)
                yield 'b'
                p.tt("dve", Sst[:, :].rearrange("p (h d) -> p h d", h=4),
                     Sst[:, :].rearrange("p (h d) -> p h d", h=4),
                     bc(cdb[:, 0:4].unsqueeze(2), [128, 4, 64]), ALU.mult)
                yield 'b'
                p.tt("dve", Sst[:, :], Sst[:, :], c.st_ps[:, :], ALU.add)
                yield 'b'
                p.cp("act", Sbf[:, :], Sst[:, :])
                yield 1
                for h in range(4):
                    i2 = h % 2
                    p.ts("dve", c.Lh[i2][:, :], maskSU, dtt[:, 4 + h:5 + h], None, ALU.mult)
                    yield 'c'
                    p.mm(c.seg_ps[i2][:, :], c.Lh[i2][:, :], tri, True, True)
                    yield 'c'
                    p.act(c.dec[i2][:, :], c.seg_ps[i2][:, :], AF.Exp)
                    yield 'c'
                    p.tt("dve", c.Mh[i2][:, :], c.dec[i2][:, :], c.cbm[:, :], ALU.mult)
                    yield 'c'
                    p.mm(c.yd_ps[:, h * 64:(h + 1) * 64], c.Mh[i2][:, :], c.xd[:, h * 64:(h + 1) * 64], True, True)
                    yield 'c'
                y1, y3 = c.y1, c.y3
                p.tt("dve", y1[:, :].rearrange("p (h d) -> p h d", h=4),
                     c.yo_ps[:, :].rearrange("p (h d) -> p h d", h=4),
                     bc(dtt[:, 12:16].unsqueeze(2), [64, 4, 64]), ALU.mult)
                p.tt("pool", y3[:, :].rearrange("p (h d) -> p h d", h=4), x3,
                     bc(bca[:, 8:12].unsqueeze(2), [64, 4, 64]), ALU.mult)
                yield 'c'
                p.tt("dve", y1[:, :], y1[:, :], c.yd_ps[:, :], ALU.add)
                yield 'c'
                p.tt("dve", y1[:, :], y1[:, :], y3[:, :], ALU.add)
                yield 'c'
                p.tt("dve", y1[:, :], y1[:, :], c.sz[:, :], ALU.mult)
                yield 'c'
                p.tt("dve", y3[:, :], y1[:, :], y1[:, :], ALU.mult)
                yield 'c'
                p.red("dve", c.ssy[:, 0:1], y3[:, :])
                yield 'c'
                p.act(c.ssy[:, 1:2], c.ssy[:, 0:1], AF.Ln, bias=eps_t[0:64, 0:1], scale=1.0 / 256)
                yield 'c'
                p.ts("dve", c.ssy[:, 3:4], c.ssy[:, 1:2], -0.5, None, ALU.mult)
                yield 'c'
                p.act(c.ssy[:, 2:3], c.ssy[:, 3:4], AF.Exp)
                yield 'c'
                p.ts("dve", y3[:, :], y1[:, :], c.ssy[:, 2:3], None, ALU.mult)
                yield 'c'
                p.tt("dve", c.y6[:, :], y3[:, :], bca[:, 256:512], ALU.mult)
                yield 'c'
                p.tr(c.yT_ps[:, 0, :], c.y6[:, 0:128], ident[0:64, 0:64])
                p.tr(c.yT_ps[:, 1, :], c.y6[:, 128:256], ident[0:64, 0:64])
                yield 'c'
                p.cp("act", yaT[:, :, tk], c.yT_ps[:, :, :])
                yield 1

            def store_tile(T):
                yaT = yaT2[T % 2]
                for hh in range(2):
                    c0 = (T % 4) * NOWN + (T // 4) * TILE
                    p.dma("pool", yaT_in[c0 // CW][hh * 128:(hh + 1) * 128, c0 % CW:c0 % CW + TILE], yaT[:, hh, :])

            cur = [0]

            def lane(q):
                for cg in range(q, NT * 8, 2):
                    T = cg // 8
                    if cg % 8 == q:
                        pro_finish(T)
                        if q == 1:
                            cur[0] = T + 1
                            pro_start(T + 1)
                    for v_ in chunk(cg, sets[q]):
                        yield v_
                        if q == 1:
                            pro_step(cur[0], 1)
                    if q == 1 and cg % 8 == 7:
                        store_tile(T)

            lanes = [lane(0), lane(1)]
            LAG = 24
            alive = [True, True]

            FINE = set('abc')

            def step(q):
                while alive[q]:
                    try:
                        v = next(lanes[q])
                    except StopIteration:
                        alive[q] = False
                        return
                    if v == 1 or v in FINE:
                        return
            SEQ_DEBUG = False
            LAG = 20
            if SEQ_DEBUG:
                for cg in range(NT * 8):
                    if cg % 8 == 0:
                        prologue(cg // 8)
                    for _ in chunk(cg, sets[cg % 2]):
                        pass
                    if cg % 8 == 7:
                        store_tile(cg // 8)
            else:
                for _ in range(LAG):
                    step(0)
                while alive[0] or alive[1]:
                    step(0)
                    step(1)
            for i_ in range(NCA):
                p.collective("AllGather", groups, yaT_in[i_], yaT_all[i_])
            if debug:
                for i_ in range(NCA):
                    p.dma("pool", dbg["ya"][:, i_ * CW:(i_ + 1) * CW], yaT_all[i_][:, :])
            p.barrier()
            p.flush(final=(stop_after == "A"))
        if stop_after == "A":
            return nc

        with contextlib.ExitStack() as stO:
            ao = Alloc(nc, stO)
            ybT = ao.sb("ybT", [128, 8, NOWN], BF16)
            flg = ao.sb("flg", [128, 8], F32)
            p.dma("sp", flg[:, :], flags[:, :])
            with contextlib.ExitStack() as st:
                a = Alloc(nc, st)
                Wc = a.sb("Wc", [128, 8, 3072], BF16)
                stg = [a.sb("stgB%d" % i_, [128, 1024], F32) for i_ in range(2)]
                dww = a.sb("dww", [128, 8 * 31], F32)
                xt = [a.sb("xtB%d" % i, [128, D], F32) for i in range(2)]
                xn = a.sb("xnB", [128, D], BF16)
                junk = a.sb("junkB", [128, D], F32)
                ss = a.sb("ssB", [128, 4], F32)
                hnT = a.sb("hnTB", [128, 8, TILE + HALO], BF16)
                sg = a.sb("sgB", [128, TILE + HALO], F32)
                u = a.sb("uB", [128, TILE + HALO], F32)
                uc = a.sb("ucB", [128, 8, TILE], F32)
                acc2 = a.sb("acc2B", [128, TILE], F32)
                accp = a.sb("accpB", [128, TILE], F32)
                ucb = a.sb("ucbB", [128, TILE], BF16)
                usq = a.sb("usqB", [128, TILE], BF16)
                onesb = a.sb("onesB", [128, 128], BF16)
                mean = a.sb("meanB", [128, TILE], F32)
                rstd = a.sb("rstdB", [128, TILE], F32)
                tmpv = a.sb("tmpvB", [128, TILE], F32)
                t2 = a.sb("t2B", [128, TILE], F32)
                szb = a.sb("szbB", [128, TILE], F32)
                tp_ps = a.ps("tp_psB", [128, 8, 128], BF16)
                vm_ps = a.ps("vm_psB", [128, TILE], F32)
                gm_ps = a.ps("gm_psB", [128, TILE], F32)
                h_ps = a.ps("h_psB", [128, 512], F32)
                sum_ps = a.ps("sum_psB", [128, TILE], F32)
                sq_ps = a.ps("sq_psB", [128, TILE], F32)
                zb_ps = a.ps("zb_psB", [128, TILE], F32)
                for k in range(8):
                    for c3 in range(3):
                        load_weight_bf16(p, "sp", Wc, w_conf, k, 1024, stg, vec_sb[:, k:k + 1],
                                         col0=c3 * 1024, dst_col0=c3 * 1024)
                p.dma("sp", dww[:, :], dw_w[:, :])
                p.memset("dve", onesb[:, :], 1.0)
                for i in range(NTL):
                    p.dma("sp", xt[0][0:HALO, :], xo[i, 0:HALO, :])
                    front_end(p, xt[0], junk, ss, xn, tp_ps, hnT, 0, ident, ntok=HALO)
                    for uu in range(4):
                        x_t = xt[(uu + 1) % 2]
                        p.dma("sp", x_t[:, :], xo[i, HALO + uu * 128:HALO + (uu + 1) * 128, :])
                        front_end(p, x_t, junk, ss, xn, tp_ps, hnT, HALO + uu * 128, ident)
                    for ct in range(8):
                        cv = slice(ct * 128, (ct + 1) * 128)
                        cg_ = slice(1024 + ct * 128, 1024 + (ct + 1) * 128)
                        for k in range(8):
                            p.mm(vm_ps[:, :], Wc[:, k, cv], hnT[:, k, HALO:], k == 0, k == 7)
                        for k in range(8):
                            p.mm(gm_ps[:, :], Wc[:, k, cg_], hnT[:, k, HALO:], k == 0, k == 7)
                        for k in range(8):
                            p.mm(h_ps[:, 0:HALO], Wc[:, k, cv], hnT[:, k, 0:HALO], k == 0, k == 7)
                        for k in range(8):
                            p.mm(h_ps[:, HALO:2 * HALO], Wc[:, k, cg_], hnT[:, k, 0:HALO], k == 0, k == 7)
                        p.act(sg[:, HALO:], gm_ps[:, :], AF.Sigmoid)
                        p.act(sg[:, 0:HALO], h_ps[:, HALO:2 * HALO], AF.Sigmoid)
                        p.tt("dve", u[:, HALO:], vm_ps[:, :], sg[:, HALO:], ALU.mult)
                        p.tt("dve", u[:, 0:HALO], h_ps[:, 0:HALO], sg[:, 0:HALO], ALU.mult)
                        wv = ct * 31
                        p.ts("dve", uc[:, ct, :], u[:, 2:2 + TILE], dww[:, wv:wv + 1], vec_sb[:, 28 + ct:29 + ct],
                             ALU.mult, ALU.add)
                        for j in range(1, 31):
                            p.stt("dve", uc[:, ct, :], u[:, 2 + j:2 + j + TILE], dww[:, wv + j:wv + j + 1],
                                  uc[:, ct, :], ALU.mult, ALU.add)
                        p.cp("act", ucb[:, :], uc[:, ct, :])
                        p.act(usq[:, :], uc[:, ct, :], AF.Square)
                        p.mm(sum_ps[:, :], onesb[:, :], ucb[:, :], ct == 0, ct == 7)
                        p.mm(sq_ps[:, :], onesb[:, :], usq[:, :], ct == 0, ct == 7)
                    p.act(mean[:, :], sum_ps[:, :], AF.Copy, scale=1.0 / 1024)
                    p.act(rstd[:, :], sq_ps[:, :], AF.Copy, scale=1.0 / 1024)
                    p.tt("dve", tmpv[:, :], mean[:, :], mean[:, :], ALU.mult)
                    p.tt("dve", rstd[:, :], rstd[:, :], tmpv[:, :], ALU.subtract)
                    p.act(rstd[:, :], rstd[:, :], AF.Sqrt, bias=eps_t[:, 1:2])
                    p.recip("dve", rstd[:, :], rstd[:, :])
                    for ct in range(8):
                        cz = slice(2048 + ct * 128, 2048 + (ct + 1) * 128)
                        for k in range(8):
                            p.mm(zb_ps[:, :], Wc[:, k, cz], hnT[:, k, HALO:], k == 0, k == 7)
                        p.tt("dve", tmpv[:, :], uc[:, ct, :], mean[:, :], ALU.subtract)
                        p.tt("dve", tmpv[:, :], tmpv[:, :], rstd[:, :], ALU.mult)
                        p.act(t2[:, :], tmpv[:, :], AF.Silu, bias=vec_sb[:, 44 + ct:45 + ct],
                              scale=vec_sb[:, 36 + ct:37 + ct])
                        p.act(szb[:, :], zb_ps[:, :], AF.Silu)
                        p.tt("dve", ybT[:, ct, i * TILE:(i + 1) * TILE], t2[:, :], szb[:, :], ALU.mult)
                if debug:
                    for ct in range(8):
                        p.dma("pool", dbg["yb"][ct * 128:(ct + 1) * 128, :], ybT[:, ct, :])
                p.barrier()
                p.flush(final=(stop_after == "B"))
            if stop_after == "B":
                return nc
            with contextlib.ExitStack() as st:
                a = Alloc(nc, st)
                Wo = a.sb("Wo0", [128, 16, D], BF16)
                stg = [a.sb("stgC%d" % i_, [128, 1024], F32) for i_ in range(2)]

                class CSet:
                    pass
                csets = []
                for q in range(2):
                    c = CSet()
                    c.cand = [a.sb("candC%d_%d" % (i, q), [128, 8, 128], BF16) for i in range(4)]
                    c.yT = a.sb("yTC%d" % q, [128, 8, 128], BF16)
                    c.yTf = a.sb("yTfC%d" % q, [128, 8, 128], F32)
                    c.x_t = a.sb("xtC%d" % q, [128, D], F32)
                    c.x1t = a.sb("x1tC%d" % q, [128, D], F32)
                    c.xn = a.sb("xnC%d" % q, [128, D], BF16)
                    c.junk = a.sb("junkC%d" % q, [128, D], F32)
                    c.ss = a.sb("ssC%d" % q, [128, 4], F32)
                    c.tp_ps = a.ps("tp_psC%d" % q, [128, 8, 128], BF16)
                    c.o_ps = [a.ps("o_psC%d_%d" % (i, q), [128, 512], F32) for i in range(2)]
                    csets.append(c)
                for k in range(16):
                    load_weight_bf16(p, "sp", Wo, w_out0, k, 1024, stg, None)

                def tileC(t, c):
                    i, uu = t // 4, t % 4
                    ybt = R(ybT[:, :, t * 128:(t + 1) * 128], "ybT_%d" % t)
                    for r in range(4):
                        c0_ = r * NOWN + t * 128
                        p.dma("sp" if r % 2 == 0 else "pool", c.cand[r][:, :, :],
                              yaT_all[c0_ // CW][:, c0_ % CW:c0_ % CW + 128].rearrange("(k p) t -> p k t", p=128))
                    p.dma("sp", c.x_t[:, :], xo[i, HALO + uu * 128:HALO + (uu + 1) * 128, :])
                    yield
                    p.ts("dve", c.yTf[:, :, :], c.cand[0][:, :, :], flg[:, 0:1], None, ALU.mult)
                    yield
                    for r in range(1, 4):
                        p.stt("dve", c.yTf[:, :, :], c.cand[r][:, :, :], flg[:, r:r + 1], c.yTf[:, :, :], ALU.mult, ALU.add)
                        yield
                    p.cp("dve", c.yT[:, :, :], c.yTf[:, :, :])
                    yield
                    for dh in range(2):
                        for kc in range(8):
                            p.mm(c.o_ps[dh][:, :], c.yT[:, kc, :], Wo[:, kc, dh * 512:(dh + 1) * 512], kc == 0, False)
                        for kc in range(8):
                            p.mm(c.o_ps[dh][:, :], ybt[:, kc, :],
                                 Wo[:, 8 + kc, dh * 512:(dh + 1) * 512], False, kc == 7)
                        yield
                        p.tt("dve", c.x1t[:, dh * 512:(dh + 1) * 512], c.o_ps[dh][:, :], c.x_t[:, dh * 512:(dh + 1) * 512], ALU.add)
                        yield
                    p.dma("pool", x1s[t * 128:(t + 1) * 128, :], c.x1t[:, :])
                    yield from front_end_g(p, c.x1t, c.junk, c.ss, c.xn, c.tp_ps, ybt, 0, ident)

                def laneC(q):
                    for t in range(q, NTL * 4, 2):
                        yield from tileC(t, csets[q])
                run_lanes([laneC(0), laneC(1)], 8)
                p.barrier()
                p.flush(final=(stop_after == "C"))
            if stop_after == "C":
                return nc

            with contextlib.ExitStack() as st:
                a = Alloc(nc, st)
                NT4 = NTL * 4
                W1 = a.sb("W1", [128, 8, 4096], BF16)
                stg = [a.sb("stgD%d" % i_, [128, 1024], F32) for i_ in range(2)]
                posf = a.sb("posf", [128, NT4], F32)
                ang = a.sb("ang", [128, NT4, 8], F32)
                tm1 = a.sb("tm1D", [128, NT4, 8], F32)
                tm2 = a.sb("tm2D", [128, NT4, 8], F32)
                tm3 = a.sb("tm3D", [128, NT4, 8], F32)
                cosT = a.sb("cosT", [128, NT4, 8], F32)
                sinT = a.sb("sinT", [128, NT4, 8], F32)

                class DSet:
                    pass
                dsets = []
                for q in range(2):
                    c = DSet()
                    c.qk = a.sb("qkD%d" % q, [128, 2048], F32)
                    c.sq = a.sb("sqD%d" % q, [128, 2048], BF16)
                    c.ssq = a.sb("ssqD%d" % q, [128, 96], F32)
                    c.qkb = a.sb("qkbD%d" % q, [128, 2048], BF16)
                    c.vt = a.sb("vtD%d" % q, [128, 1024], BF16)
                    c.gt = a.sb("gtD%d" % q, [128, 1024], BF16)
                    c.rA = a.sb("rA%d" % q, [128, 32, 8], F32)
                    c.rB = a.sb("rB%d" % q, [128, 32, 8], F32)
                    c.rA2 = a.sb("rA2%d" % q, [128, 32, 8], F32)
                    c.rB2 = a.sb("rB2%d" % q, [128, 32, 8], F32)
                    c.QTt = a.sb("QTt%d" % q, [128, 8, 128], BF16)
                    c.KTt = a.sb("KTt%d" % q, [128, 8, 128], BF16)
                    c.ip_ps = [a.ps("ip_psD%d_%d" % (i, q), [128, 512], F32) for i in range(2)]
                    c.tq_ps = a.ps("tq_psD%d" % q, [128, 8, 128], BF16)
                    c.tk_ps = a.ps("tk_psD%d" % q, [128, 8, 128], BF16)
                    dsets.append(c)
                for k in range(8):
                    for c4 in range(4):
                        load_weight_bf16(p, "sp", W1, w_in1, k, 1024, stg, vec_sb[:, 52 + k:53 + k],
                                         col0=c4 * 1024, dst_col0=c4 * 1024)
                p.dma("sp", posf[:, :], pos[:, :])
                invf = cst_sb[:, 448:456]
                for t in range(NT4):
                    p.ts("dve", ang[:, t, :], invf, posf[:, t:t + 1], None, ALU.mult)
                p.ts("pool", tm1[:, :, :], ang[:, :, :], 1.0 / (2 * math.pi), None, ALU.mult)
                p.ts("dve", tm2[:, :, :], tm1[:, :, :], 12582912.0, None, ALU.add)
                p.ts("pool", tm3[:, :, :], tm2[:, :, :], -12582912.0, None, ALU.add)
                p.tt("dve", ang[:, :, :], tm1[:, :, :], tm3[:, :, :], ALU.subtract)
                p.act(sinT[:, :, :], ang[:, :, :], AF.Sin, scale=2 * math.pi)
                p.act(tm2[:, :, :], ang[:, :, :], AF.Sin, scale=math.pi)
                p.tt("dve", tm3[:, :, :], tm2[:, :, :], tm2[:, :, :], ALU.mult)
                p.ts("pool", cosT[:, :, :], tm3[:, :, :], -2.0, 1.0, ALU.mult, ALU.add)

                def tileD(t, c):
                    ybt = R(ybT[:, :, t * 128:(t + 1) * 128], "ybT_%d" % t)
                    qk, sq, ssq, qkb, vt, gt = c.qk, c.sq, c.ssq, c.qkb, c.vt, c.gt
                    for cg in range(8):
                        ips = c.ip_ps[cg % 2]
                        for k in range(8):
                            p.mm(ips[:, :], ybt[:, k, :], W1[:, k, cg * 512:(cg + 1) * 512], k == 0, k == 7)
                        yield
                        if cg < 4:
                            p.cp("act", qk[:, cg * 512:(cg + 1) * 512], ips[:, :])
                        elif cg < 6:
                            p.cp("act", vt[:, (cg - 4) * 512:(cg - 3) * 512], ips[:, :])
                        else:
                            p.act(gt[:, (cg - 6) * 512:(cg - 5) * 512], ips[:, :], AF.Silu)
                        yield
                    p.act(sq[:, :], qk[:, :], AF.Square)
                    yield
                    p.red("dve", ssq[:, 0:32], sq[:, :].rearrange("p (a d) -> p a d", d=64))
                    yield
                    p.act(ssq[:, 32:64], ssq[:, 0:32], AF.Ln, bias=eps_t[:, 0:1], scale=1.0 / 64)
                    yield
                    p.ts("dve", ssq[:, 0:32], ssq[:, 32:64], -0.5, None, ALU.mult)
                    yield
                    p.act(ssq[:, 64:96], ssq[:, 0:32], AF.Exp)
                    yield
                    qk3 = qk[:, :].rearrange("p (a d) -> p a d", d=64)
                    p.tt("dve", qk3, qk3, bc(ssq[:, 64:96].unsqueeze(2), [128, 32, 64]), ALU.mult)
                    yield
                    p.tt("dve", qk3[:, 0:16, :], qk3[:, 0:16, :], bc(bcc[:, 0:64].unsqueeze(1), [128, 16, 64]), ALU.mult)
                    yield
                    p.tt("dve", qk3[:, 16:32, :], qk3[:, 16:32, :], bc(bcc[:, 64:128].unsqueeze(1), [128, 16, 64]), ALU.mult)
                    yield
                    p.cp("act", qkb[:, :], qk[:, :])
                    qb3 = qkb[:, :].rearrange("p (a d) -> p a d", d=64)
                    cb_ = bc(cosT[:, t, :].unsqueeze(1), [128, 32, 8])
                    sb_ = bc(sinT[:, t, :].unsqueeze(1), [128, 32, 8])
                    p.tt("dve", c.rA[:, :, :], qk3[:, :, 0:8], cb_, ALU.mult)
                    yield
                    p.tt("dve", c.rA2[:, :, :], qk3[:, :, 8:16], cb_, ALU.mult)
                    p.tt("pool", c.rB[:, :, :], qk3[:, :, 8:16], sb_, ALU.mult)
                    yield
                    p.tt("pool", c.rB2[:, :, :], qk3[:, :, 0:8], sb_, ALU.mult)
                    yield
                    p.tt("dve", qb3[:, :, 0:8], c.rA[:, :, :], c.rB[:, :, :], ALU.subtract)
                    yield
                    p.tt("dve", qb3[:, :, 8:16], c.rA2[:, :, :], c.rB2[:, :, :], ALU.add)
                    yield
                    for h in range(8):
                        p.tr(c.tq_ps[:, h, :], qkb[:, h * 128:(h + 1) * 128], ident[:, :])
                    yield
                    for h in range(8):
                        p.tr(c.tk_ps[:, h, :], qkb[:, 1024 + h * 128:1024 + (h + 1) * 128], ident[:, :])
                    yield
                    p.cp("act", c.QTt[:, :, :], c.tq_ps[:, :, :])
                    p.cp("dve", c.KTt[:, :, :], c.tk_ps[:, :, :])
                    yield
                    tc_ = slice(t * 128, (t + 1) * 128)
                    p.dma("sp", QTs[:, tc_].rearrange("(h p) t -> p h t", p=128), c.QTt[:, :, :])
                    for h in range(8):
                        p.dma("pool" if h % 2 else "sp", kT_in[h][:, tc_], c.KTt[:, h, :])
                    p.dma("sp", v_in[t // 4][(t % 4) * 128:(t % 4 + 1) * 128, :], vt[:, :])
                    p.dma("pool", gs[tc_, :], gt[:, :])
                    yield

                def laneD(q):
                    for t in range(q, NT4, 2):
                        yield from tileD(t, dsets[q])
                run_lanes([laneD(0), laneD(1)], 12)
                for h in range(8):
                    p.collective("AllGather", groups, kT_in[h], kT_all[h])
                for c_ in range(NVC):
                    p.collective("AllGather", groups, v_in[c_], v_all[c_])
                p.barrier()
                p.flush(final=(stop_after == "D"))
            if stop_after == "D":
                return nc
        with contextlib.ExitStack() as stO:
            ao = Alloc(nc, stO)
            NT4 = NTL * 4
            ogT = ao.sb("ogT", [128, NT4, 8, 128], BF16)
            with contextlib.ExitStack() as st:
                a = Alloc(nc, st)
                KT = a.sb("KT", [128, 4, NOWN], BF16)
                V = a.sb("V", [128, 4, NT4, 132], BF16)
                QTz = [a.sb("QTz%d" % m, [128, NOWN], BF16) for m in range(2)]
                msk = a.sb("msk", [128, 16, 512], BF16)
                NPT = 5
                LA = 3
                PT = [a.sb("PT%d" % i, [128, 1024], BF16) for i in range(NPT)]
                gh2 = [a.sb("gh2_%d" % i, [128, NT4, 128], BF16) for i in range(2)]
                Osb = [[a.sb("Osb%d_%d" % (b_, j), [128, 2, 132], F32) for j in range(4)] for b_ in range(2)]
                zr = a.sb("zr", [128, 4], F32)
                o0 = a.sb("o0", [128, 128], F32)
                o1 = a.sb("o1", [128, 128], F32)
                osq = a.sb("osq", [128, 128], F32)
                sso = a.sb("sso", [128, 4], F32)
                NSP = 2
                s_ps = [a.ps("s_ps%d" % i, [128, 1024], F32) for i in range(NSP)]
                O_ps = [[a.ps("O_ps%d_%d" % (m, j), [128, 2, 132], F32) for j in range(2)] for m in range(2)]
                p.dma("sp", msk[:, :, :], masks[:, :].rearrange("p (a b) -> p a b", a=16))
                p.memset("dve", QTz[0][64:128, :], 0.0)
                p.memset("pool", QTz[1][0:64, :], 0.0)
                for r in range(4):
                    for kt_ in range(NT4):
                        p.memset("dve" if kt_ % 2 == 0 else "pool", V[:, r, kt_, 128:132], 1.0)

                def epilogue(h, i, ob, ghh):
                    for qs in range(4):
                        t = i * 4 + qs
                        Oa = ob[qs // 2]
                        Ob = ob[2 + qs // 2]
                        j = qs % 2
                        p.recip("dve", zr[:, 0:1], Oa[:, j, 128:129])
                        p.recip("dve", zr[:, 1:2], Ob[:, j, 128:129])
                        yield
                        p.tt("pool", zr[:, 2:3], zr[:, 1:2], lamt[:, 3:4], ALU.mult)
                        p.act(o0[:, :], Oa[:, j, 0:128], AF.Copy, scale=zr[:, 0:1])
                        yield
                        p.act(o1[:, :], Ob[:, j, 0:128], AF.Copy, scale=zr[:, 2:3])
                        yield
                        p.tt("dve", o0[:, :], o0[:, :], o1[:, :], ALU.add)
                        yield
                        p.tt("pool", osq[:, :], o0[:, :], o0[:, :], ALU.mult)
                        yield
                        p.red("dve", sso[:, 0:1], osq[:, :])
                        yield
                        p.act(sso[:, 1:2], sso[:, 0:1], AF.Ln, bias=eps_t[:, 0:1], scale=1.0 / 128)
                        yield
                        p.ts("dve", sso[:, 3:4], sso[:, 1:2], -0.5, None, ALU.mult)
                        yield
                        p.act(sso[:, 2:3], sso[:, 3:4], AF.Exp)
                        yield
                        p.ts("dve", o1[:, :], o0[:, :], sso[:, 2:3], None, ALU.mult)
                        yield
                        p.tt("dve", o1[:, :], o1[:, :], bcc[:, 384:512], ALU.mult)
                        yield
                        p.tt("pool", ogT[:, t, h, :], o1[:, :], ghh[:, t, :], ALU.mult)
                        yield

                pending = []

                def advance(n):
                    for _ in range(n):
                        if not pending:
                            return
                        try:
                            next(pending[0])
                        except StopIteration:
                            pending.pop(0)

                kglob = 0
                blk = 0
                for h in range(8):
                    ghh = gh2[h % 2]
                    for r in range(4):
                        p.dma("sp", KT[:, r, :], kT_all[h][r * 128:(r + 1) * 128, :])
                        for c_ in range(NVC):
                            p.dma("pool" if c_ % 2 == 0 else "sp", V[:, r, c_ * 4:(c_ + 1) * 4, 0:128],
                                  v_all[c_][r * VR:(r + 1) * VR, h * 128:(h + 1) * 128].rearrange("(t p) d -> p t d", p=128))
                    p.dma("sp", QTz[0][0:64, :], QTs[h * 128:h * 128 + 64, :])
                    p.dma("pool", QTz[1][64:128, :], QTs[h * 128 + 64:(h + 1) * 128, :])
                    p.dma("pool", ghh[:, :, :], gs[:, h * 128:(h + 1) * 128].rearrange("(t p) d -> p t d", p=128))
                    for i in range(NTL):
                        keys = [(r, ip * 4 + uu, None) for r in range(4) for ip in range(i) for uu in range(4)]
                        keys += [(r, i * 4 + uu, r * 4 + uu) for r in range(4) for uu in range(4)]
                        tiles = [(m, idx, r, kt, mk) for idx, (r, kt, mk) in enumerate(keys) for m in range(2)]
                        n = len(tiles)
                        LA2 = 4
                        for k0 in range(0, n + LA2, 2):
                            if k0 < n:
                                pi = ((kglob + k0) // 2)
                                spair = s_ps[pi % NSP]
                                ptp = PT[pi % NPT]
                                m0, idx0, r, kt, mk = tiles[k0]
                                for hf in range(2):
                                    p.mm(spair[:, hf * 512:(hf + 1) * 512], KT[:, r, kt * 128:(kt + 1) * 128],
                                         QTz[hf][:, i * TILE:(i + 1) * TILE], True, True)
                                p.act(ptp[:, :], spair[:, :], AF.Exp, bias=negM[:, 0:1], scale=0.125)
                                if mk is not None:
                                    p.tt("dve", ptp[:, :].rearrange("p (a b) -> p a b", a=2),
                                         ptp[:, :].rearrange("p (a b) -> p a b", a=2),
                                         bc(msk[:, mk, :].unsqueeze(1), [128, 2, 512]), ALU.mult)
                            kp = k0 - LA2
                            if 0 <= kp < n:
                                pi = ((kglob + kp) // 2)
                                ptp = PT[pi % NPT]
                                m0, idx, r, kt, mk = tiles[kp]
                                for hf in range(2):
                                    for qs in range(4):
                                        p.mm(O_ps[hf][qs // 2][:, qs % 2, 0:129],
                                             ptp[:, hf * 512 + qs * 128:hf * 512 + (qs + 1) * 128],
                                             V[:, r, kt, 0:129], idx == 0 and qs % 2 == 0, idx == len(keys) - 1)
                            advance(4)
                        kglob += n
                        while pending:
                            advance(1)
                        ob = Osb[blk % 2]
                        blk += 1
                        p.cp("act", ob[0][:, :, :], O_ps[0][0][:, :, :])
                        p.cp("dve", ob[1][:, :, :], O_ps[0][1][:, :, :])
                        p.cp("act", ob[2][:, :, :], O_ps[1][0][:, :, :])
                        p.cp("dve", ob[3][:, :, :], O_ps[1][1][:, :, :])
                        pending.append(epilogue(h, i, ob, ghh))
                while pending:
                    advance(1)
                p.barrier()
                p.flush(final=(stop_after == "E1"))
            if stop_after == "E1":
                return nc
            with contextlib.ExitStack() as st:
                a = Alloc(nc, st)
                Wo1 = a.sb("Wo1", [128, 8, D], BF16)
                stg = [a.sb("stgF%d" % i_, [128, 1024], F32) for i_ in range(2)]
                x1t = [a.sb("x1tF%d" % i, [128, D], F32) for i in range(2)]
                ot = [a.sb("otF%d" % i, [128, D], F32) for i in range(2)]
                f_ps = [a.ps("f_ps%d" % i, [128, 512], F32) for i in range(2)]
                tf_ps = [a.ps("tf_ps%d" % i, [128, 8, 128], BF16) for i in range(2)]
                ogTt = [a.sb("ogTt%d" % i, [128, 8, 128], BF16) for i in range(2)]
                for k in range(8):
                    load_weight_bf16(p, "sp", Wo1, w_out1, k, 1024, stg, None)
                for t in range(NT4):
                    xx = x1t[t % 2]
                    oo = ot[t % 2]
                    p.dma("sp", xx[:, :], x1s[t * 128:(t + 1) * 128, :])
                    for h in range(8):
                        p.tr(tf_ps[t % 2][:, h, :], ogT[:, t, h, :], ident[:, :])
                    p.cp("act", ogTt[t % 2][:, :, :], tf_ps[t % 2][:, :, :])
                    for dh in range(2):
                        for h in range(8):
                            p.mm(f_ps[dh][:, :], ogTt[t % 2][:, h, :], Wo1[:, h, dh * 512:(dh + 1) * 512],
                                 h == 0, h == 7)
                        p.tt("dve", oo[:, dh * 512:(dh + 1) * 512], f_ps[dh][:, :], xx[:, dh * 512:(dh + 1) * 512], ALU.add)
                    p.dma("pool", out[t * 128:(t + 1) * 128, :], oo[:, :])
                p.flush(final=True)
    return nc


def _consts():
    c = np.zeros((128, 1024), np.float32)
    c[:, 0:128] = np.eye(128, dtype=np.float32)
    j = np.arange(64)[:, None]
    l = np.arange(64)[None, :]
    c[0:64, 128:192] = (j <= l)
    c[0:64, 192:320] = 1.0
    c[0:64, 320:384] = (j > l)
    c[0:64, 384:448] = (j <= l)
    c[:, 448:456] = (500000.0 ** (-2.0 * np.arange(8, dtype=np.float32) / 16.0)).astype(np.float32)[None, :]
    return c


def prep_inputs(inp, S):
    NT = S // TILE
    NTL = NT // 4
    x = np.asarray(inp["x"], np.float32)
    w_in0 = np.asarray(inp["a_w_in"], np.float32)[0]
    conv_w = np.asarray(inp["a_conv_w"], np.float32)[0]
    conv_b = np.asarray(inp["a_conv_b"], np.float32)[0]
    maps = []
    cst = _consts()
    for c in range(8):
        b, s = c // 4, c % 4
        g = s
        m = {}
        m["xb"] = np.ascontiguousarray(x[b, :S])
        xo = np.zeros((NTL, TILE + HALO, D), np.float32)
        for i in range(NTL):
            jt = 4 * i + s
            lo = jt * TILE - HALO
            if lo < 0:
                xo[i, HALO:] = x[b, 0:TILE]
            else:
                xo[i] = x[b, lo:lo + TILE + HALO]
        m["xo"] = xo
        posb = np.asarray(inp["positions"])[b]
        own = np.concatenate([posb[(4 * i + s) * TILE:(4 * i + s + 1) * TILE] for i in range(NTL)])
        m["pos"] = np.ascontiguousarray(own.reshape(NTL * 4, 128).T.astype(np.float32))
        m["cst"] = cst
        cols = np.concatenate([np.arange(g * 256, (g + 1) * 256),
                               1024 + np.arange(g * 256, (g + 1) * 256),
                               2048 + np.arange(g * 128, (g + 1) * 128),
                               2560 + np.arange(g * 128, (g + 1) * 128),
                               3072 + np.arange(g * 4, (g + 1) * 4)])
        m["w_ssd"] = np.ascontiguousarray(w_in0[:, cols])
        va = np.zeros((128, 64), np.float32)
        va[:, 0:8] = np.asarray(inp["a_norm_w"], np.float32)[0].reshape(8, 128).T
        ccols = [np.arange(g * 256, g * 256 + 128), np.arange(g * 256 + 128, (g + 1) * 256),
                 1024 + np.arange(g * 128, (g + 1) * 128), 1536 + np.arange(g * 128, (g + 1) * 128)]
        for ct in range(4):
            va[:, 8 + ct * 4:12 + ct * 4] = conv_w[:, ccols[ct]].T
            va[:, 24 + ct] = conv_b[ccols[ct]]
        va[:, 28:36] = np.asarray(inp["a_dw_b"], np.float32)[0].reshape(8, 128).T
        va[:, 36:44] = np.asarray(inp["a_ln_w"], np.float32)[0].reshape(8, 128).T
        va[:, 44:52] = np.asarray(inp["a_ln_b"], np.float32)[0].reshape(8, 128).T
        va[:, 52:60] = np.asarray(inp["c_norm_w"], np.float32)[0].reshape(8, 128).T
        m["vec_a"] = va
        ba = np.zeros((64, 512), np.float32)
        ba[:, 0:4] = np.asarray(inp["a_dt_bias"], np.float32)[0][g * 4:(g + 1) * 4][None, :]
        ba[:, 4:8] = np.asarray(inp["a_a_log"], np.float32)[0][g * 4:(g + 1) * 4][None, :]
        ba[:, 8:12] = np.asarray(inp["a_d_skip"], np.float32)[0][g * 4:(g + 1) * 4][None, :]
        ba[:, 256:512] = np.asarray(inp["a_ssd_norm_w"], np.float32)[0][g * 256:(g + 1) * 256][None, :]
        m["bc_a"] = ba
        m["w_conf"] = np.ascontiguousarray(w_in0[:, 3088:6160])
        dww = np.asarray(inp["a_dw_w"], np.float32)[0]
        m["dw_w"] = np.ascontiguousarray(dww.T.reshape(8, 128, 31).transpose(1, 0, 2).reshape(128, 248))
        m["w_out0"] = np.asarray(inp["a_w_out"], np.float32)[0]
        m["w_in1"] = np.asarray(inp["c_w_in"], np.float32)[0]
        bcv = np.concatenate([np.asarray(inp[n], np.float32)[0] for n in
                              ("c_q_norm_w", "c_k_norm_w", "c_lq1", "c_lk1", "c_lq2", "c_lk2", "c_subln_w")])
        m["bc_c"] = np.ascontiguousarray(np.broadcast_to(bcv[None, :], (128, 512)))
        m["w_out1"] = np.asarray(inp["c_w_out"], np.float32)[0]
        mk = np.zeros((128, 16, 512), np.float32)
        kk = np.arange(128)[:, None] // 64
        qq = np.arange(512)[None, :] // 64
        for r in range(4):
            for uu in range(4):
                mk[:, r * 4 + uu, :] = (r * 8 + 2 * uu + kk <= s * 8 + qq)
        m["masks"] = np.ascontiguousarray(mk.reshape(128, 16 * 512).astype(ml_dtypes.bfloat16))
        fl = np.zeros((128, 8), np.float32)
        fl[:, s] = 1.0
        m["flags"] = fl
        maps.append(m)
    return maps


def kernel(**inputs):
    S = inputs["x"].shape[1]
    nc = build(S)
    maps = prep_inputs(inputs, S)
    res = run_bass_kernel_spmd(nc, maps, core_ids=list(range(8)))
    NT = S // TILE
    NTL = NT // 4
    outp = np.zeros((2, S, D), np.float32)
    for c in range(8):
        b, s = c // 4, c % 4
        o = res.results[c]["out"]
        for i in range(NTL):
            jt = 4 * i + s
            outp[b, jt * TILE:(jt + 1) * TILE] = o[i * TILE:(i + 1) * TILE]
    return outp
```
